# Optimizing a Trainium2 kernel written in Bass

```python
import math
import jax, jax.numpy as jnp
from jax import lax
import numpy as np

D_MODEL = 1024
BATCH = 8
SEQ = 8192
DEPTH = 2
DEC_BATCH = 8
DEC_SEQ = 2048
PAST_LEN = 128

R_HEADS = 4
R_DK = 128
R_DV = 256
D_HEADS = 4
D_DH = 128
D_DV = 256
RET_QK = R_HEADS * R_DK
RET_V = R_HEADS * R_DV
DIFF_QK = D_HEADS * 2 * D_DH
DIFF_V = D_HEADS * D_DV
IN_WIDTH = 2 * RET_QK + 2 * RET_V + 2 * DIFF_QK + DIFF_V + 2 * D_MODEL
CHUNK = 128
Q_BLOCK = 128
ROPE_THETA = 10000.0
N_GROUPS = 4
EXPERTS_PER_GROUP = 8
N_EXPERTS = N_GROUPS * EXPERTS_PER_GROUP
TOP_K = 2
D_FF_EXPERT = 512
ALPHA = (2.0 * DEPTH) ** 0.25
BETA = (8.0 * DEPTH) ** -0.25
EPS = 1e-5

kernel_name = "hybrid_retention_diffattn_hmoe_encoder"


def layer_norm(x, g, b):
    xf = x.astype(jnp.float32)
    mu = jnp.mean(xf, axis=-1, keepdims=True)
    var = jnp.mean(jnp.square(xf - mu), axis=-1, keepdims=True)
    y = (xf - mu) * lax.rsqrt(var + EPS) * g.astype(jnp.float32) + b.astype(jnp.float32)
    return y.astype(x.dtype)


def rope_tables(s, d):
    pos = jnp.arange(s, dtype=jnp.float32)
    inv = ROPE_THETA ** (-jnp.arange(0, d, 2, dtype=jnp.float32) / d)
    ang = pos[:, None] * inv[None, :]
    return jnp.cos(ang), jnp.sin(ang)


def apply_rope(x, cos, sin):
    half = x.shape[-1] // 2
    c = cos[:, None, :]
    s = sin[:, None, :]
    xf = x.astype(jnp.float32)
    x1, x2 = xf[..., :half], xf[..., half:]
    out = jnp.concatenate([x1 * c - x2 * s, x2 * c + x1 * s], axis=-1)
    return out.astype(x.dtype)


def retention_one_direction(q, k, v, log_g, include_diag):
    b, s, h, dk = q.shape
    dv = v.shape[-1]
    n = s // CHUNK
    idx = jnp.arange(CHUNK, dtype=jnp.float32)
    diff = idx[:, None] - idx[None, :]
    mask = (diff >= 0) if include_diag else (diff > 0)
    intra = jnp.where(mask[None], jnp.exp(log_g[:, None, None] * jnp.maximum(diff, 0.0)[None]), 0.0)
    xi = jnp.exp(log_g[None, :] * (idx[:, None] + 1.0))
    zeta = jnp.exp(log_g[None, :] * (CHUNK - 1.0 - idx[:, None]))
    chunk_decay = jnp.exp(log_g * CHUNK)

    def to_chunks(t):
        return t.reshape(b, n, CHUNK, h, t.shape[-1]).swapaxes(0, 1)

    def body(state, inp):
        qc, kc, vc = inp
        sc = jnp.einsum('bihd,bjhd->bhij', qc, kc) * intra[None]
        o = jnp.einsum('bhij,bjhe->bihe', sc, vc)
        o = o + jnp.einsum('bihd,bhde->bihe', qc * xi[None, :, :, None], state)
        state = state * chunk_decay[None, :, None, None] + jnp.einsum('bjhd,bjhe->bhde', kc * zeta[None, :, :, None], vc)
        return state, o

    state0 = jnp.zeros((b, h, dk, dv), dtype=jnp.float32)
    _, o = lax.scan(body, state0, (to_chunks(q), to_chunks(k), to_chunks(v)))
    return o.swapaxes(0, 1).reshape(b, s, h, dv)


def bidirectional_retention(q, k, v, log_decay):
    log_f = -jnp.abs(log_decay[0].astype(jnp.float32))
    log_b = -jnp.abs(log_decay[1].astype(jnp.float32))
    o_f = retention_one_direction(q, k, v, log_f, True)
    o_b = retention_one_direction(q[:, ::-1], k[:, ::-1], v[:, ::-1], log_b, False)[:, ::-1]
    return o_f + o_b


def diff_attention(q1, q2, k1, k2, v, lam):
    b, s, h, d = q1.shape
    nq = s // Q_BLOCK
    scale = 1.0 / math.sqrt(d)
    qb = jnp.stack([q1, q2], axis=0).reshape(2, b, nq, Q_BLOCK, h, d).transpose(2, 0, 1, 3, 4, 5)

    def block(qblk):
        s1 = jnp.einsum('bqhd,bkhd->bhqk', qblk[0], k1).astype(jnp.float32) * scale
        s2 = jnp.einsum('bqhd,bkhd->bhqk', qblk[1], k2).astype(jnp.float32) * scale
        a = jax.nn.softmax(s1, axis=-1) - lam * jax.nn.softmax(s2, axis=-1)
        return jnp.einsum('bhqk,bkhe->bqhe', a, v.astype(jnp.float32))

    out = lax.map(block, qb)
    return out.transpose(1, 0, 2, 3, 4).reshape(b, s, h, v.shape[-1])


def hier_moe(x, w_rg, w_re, w_gu, w_dn):
    b, s, d = x.shape
    xt = x.reshape(b * s, d)
    g_prob = jax.nn.softmax((xt @ w_rg).astype(jnp.float32), axis=-1)
    gp, gsel = lax.top_k(g_prob, 1)
    e_logits = (xt @ w_re).astype(jnp.float32).reshape(-1, N_GROUPS, EXPERTS_PER_GROUP)
    e_sel = jnp.take_along_axis(e_logits, gsel[:, :, None], axis=1)[:, 0]
    e_prob = jax.nn.softmax(e_sel, axis=-1)
    ev, eidx = lax.top_k(e_prob, TOP_K)
    w = gp * ev / jnp.sum(ev, axis=-1, keepdims=True)
    eid = gsel * EXPERTS_PER_GROUP + eidx
    combine = jnp.sum(jax.nn.one_hot(eid, N_EXPERTS, dtype=jnp.float32) * w[..., None], axis=1)
    y = jnp.zeros((b * s, d), dtype=jnp.float32)
    for e in range(N_EXPERTS):
        hgu = xt @ w_gu[e]
        a, gate = hgu[:, :D_FF_EXPERT], hgu[:, D_FF_EXPERT:]
        y = y + combine[:, e:e + 1] * ((jax.nn.silu(a) * gate) @ w_dn[e]).astype(jnp.float32)
    return y.astype(x.dtype).reshape(b, s, d)


def encoder_layer(x, layer_idx, cos, sin, w_in, ret_log_decay, ret_gn_gain, diff_lambda, diff_subln_gain,
                  w_ret_branch, w_diff_branch, w_out, ln1_g, ln1_b, router_group, router_expert,
                  w_gate_up, w_down, ln2_g, ln2_b):
    b, s, _ = x.shape
    h = x @ w_in
    o = 0
    rq = h[..., o:o + RET_QK].reshape(b, s, R_HEADS, R_DK); o += RET_QK
    rk = h[..., o:o + RET_QK].reshape(b, s, R_HEADS, R_DK); o += RET_QK
    rv = h[..., o:o + RET_V].reshape(b, s, R_HEADS, R_DV); o += RET_V
    rg = h[..., o:o + RET_V]; o += RET_V
    dq = h[..., o:o + DIFF_QK].reshape(b, s, D_HEADS * 2, D_DH); o += DIFF_QK
    dk = h[..., o:o + DIFF_QK].reshape(b, s, D_HEADS * 2, D_DH); o += DIFF_QK
    dv = h[..., o:o + DIFF_V].reshape(b, s, D_HEADS, D_DV); o += DIFF_V
    gate_a = h[..., o:o + D_MODEL]; o += D_MODEL
    gate_b = h[..., o:o + D_MODEL]

    rq = apply_rope(rq, cos, sin)
    rk = apply_rope(rk, cos, sin) * (R_DK ** -0.5)
    ro = bidirectional_retention(rq, rk, rv, ret_log_decay)
    mu = jnp.mean(ro, axis=-1, keepdims=True)
    var = jnp.mean(jnp.square(ro - mu), axis=-1, keepdims=True)
    ro = ((ro - mu) * lax.rsqrt(var + EPS)).reshape(b, s, RET_V) * ret_gn_gain.astype(jnp.float32)
    ret_out = (jax.nn.silu(rg.astype(jnp.float32)) * ro).astype(x.dtype)

    dq = apply_rope(dq, cos, sin).reshape(b, s, D_HEADS, 2, D_DH)
    dk = apply_rope(dk, cos, sin).reshape(b, s, D_HEADS, 2, D_DH)
    lam_init = 0.8 - 0.6 * math.exp(-0.3 * layer_idx)
    lf = diff_lambda.astype(jnp.float32)
    lam = jnp.exp(jnp.sum(lf[0] * lf[1])) - jnp.exp(jnp.sum(lf[2] * lf[3])) + lam_init
    do = diff_attention(dq[..., 0, :], dq[..., 1, :], dk[..., 0, :], dk[..., 1, :], dv, lam)
    do = do * lax.rsqrt(jnp.mean(jnp.square(do), axis=-1, keepdims=True) + EPS) * diff_subln_gain.astype(jnp.float32)
    diff_out = (do * (1.0 - lam_init)).reshape(b, s, DIFF_V).astype(x.dtype)

    merged = jax.nn.sigmoid(gate_a) * (ret_out @ w_ret_branch) + jax.nn.sigmoid(gate_b) * (diff_out @ w_diff_branch)
    mix = merged @ w_out
    x = layer_norm(ALPHA * x + mix, ln1_g, ln1_b)

    x = layer_norm(ALPHA * x + hier_moe(x, router_group, router_expert, w_gate_up, w_down), ln2_g, ln2_b)
    return x


def trunk(x, w_in, ret_log_decay, ret_gn_gain, diff_lambda, diff_subln_gain, w_ret_branch, w_diff_branch,
          w_out, ln1_g, ln1_b, router_group, router_expert, w_gate_up, w_down, ln2_g, ln2_b):
    cos, sin = rope_tables(x.shape[1], R_DK)
    for l in range(DEPTH):
        x = encoder_layer(x, l, cos, sin, w_in[l], ret_log_decay[l], ret_gn_gain[l], diff_lambda[l],
                          diff_subln_gain[l], w_ret_branch[l], w_diff_branch[l], w_out[l], ln1_g[l], ln1_b[l],
                          router_group[l], router_expert[l], w_gate_up[l], w_down[l], ln2_g[l], ln2_b[l])
    return x


def setup_inputs(seed: int = 0) -> dict:
    key = jax.random.key(seed)
    ks = jax.random.split(key, 24)
    f32 = jnp.float32
    nrm = lambda k, shape, scale: jax.random.normal(k, shape, f32) * scale
    x_prompt = nrm(ks[0], (BATCH, SEQ, D_MODEL), 1.0)
    x_sample = nrm(ks[1], (DEC_BATCH, DEC_SEQ, D_MODEL), 1.0)
    sd = D_MODEL ** -0.5
    w_in = jnp.concatenate([
        nrm(ks[2], (DEPTH, D_MODEL, 2 * RET_QK), sd),
        nrm(ks[3], (DEPTH, D_MODEL, RET_V), sd * BETA),
        nrm(ks[4], (DEPTH, D_MODEL, RET_V), sd),
        nrm(ks[5], (DEPTH, D_MODEL, 2 * DIFF_QK), sd),
        nrm(ks[6], (DEPTH, D_MODEL, DIFF_V), sd * BETA),
        nrm(ks[7], (DEPTH, D_MODEL, 2 * D_MODEL), sd),
    ], axis=-1)
    base = jnp.log(1.0 - 2.0 ** (-5.0 - jnp.arange(R_HEADS, dtype=f32)))
    ret_log_decay = base[None, None, :] * (1.0 + 0.1 * jax.random.uniform(ks[8], (DEPTH, 2, R_HEADS), f32))
    ret_gn_gain = 1.0 + nrm(ks[9], (DEPTH, RET_V), 0.02)
    diff_lambda = nrm(ks[10], (DEPTH, 4, D_DH), 0.1)
    diff_subln_gain = 1.0 + nrm(ks[11], (DEPTH, D_DV), 0.02)
    w_ret_branch = nrm(ks[12], (DEPTH, RET_V, D_MODEL), RET_V ** -0.5 * BETA)
    w_diff_branch = nrm(ks[13], (DEPTH, DIFF_V, D_MODEL), DIFF_V ** -0.5 * BETA)
    w_out = nrm(ks[14], (DEPTH, D_MODEL, D_MODEL), sd * BETA)
    ln1_g = 1.0 + nrm(ks[15], (DEPTH, D_MODEL), 0.02)
    ln1_b = nrm(ks[16], (DEPTH, D_MODEL), 0.02)
    router_group = nrm(ks[17], (DEPTH, D_MODEL, N_GROUPS), sd)
    router_expert = nrm(ks[18], (DEPTH, D_MODEL, N_EXPERTS), sd)
    w_gate_up = nrm(ks[19], (DEPTH, N_EXPERTS, D_MODEL, 2 * D_FF_EXPERT), sd * BETA)
    w_down = nrm(ks[20], (DEPTH, N_EXPERTS, D_FF_EXPERT, D_MODEL), D_FF_EXPERT ** -0.5 * BETA)
    ln2_g = 1.0 + nrm(ks[21], (DEPTH, D_MODEL), 0.02)
    ln2_b = nrm(ks[22], (DEPTH, D_MODEL), 0.02)
    return {"x_prompt": x_prompt, "x_sample": x_sample, "w_in": w_in, "ret_log_decay": ret_log_decay,
            "ret_gn_gain": ret_gn_gain, "diff_lambda": diff_lambda, "diff_subln_gain": diff_subln_gain,
            "w_ret_branch": w_ret_branch, "w_diff_branch": w_diff_branch, "w_out": w_out,
            "ln1_g": ln1_g, "ln1_b": ln1_b, "router_group": router_group, "router_expert": router_expert,
            "w_gate_up": w_gate_up, "w_down": w_down, "ln2_g": ln2_g, "ln2_b": ln2_b}


def reference(x_prompt, x_sample, w_in, ret_log_decay, ret_gn_gain, diff_lambda, diff_subln_gain,
              w_ret_branch, w_diff_branch, w_out, ln1_g, ln1_b, router_group, router_expert,
              w_gate_up, w_down, ln2_g, ln2_b):
    y_prompt = trunk(x_prompt, w_in, ret_log_decay, ret_gn_gain, diff_lambda, diff_subln_gain, w_ret_branch,
                     w_diff_branch, w_out, ln1_g, ln1_b, router_group, router_expert, w_gate_up, w_down,
                     ln2_g, ln2_b)
    y_sample = trunk(x_sample, w_in, ret_log_decay, ret_gn_gain, diff_lambda, diff_subln_gain, w_ret_branch,
                     w_diff_branch, w_out, ln1_g, ln1_b, router_group, router_expert, w_gate_up, w_down,
                     ln2_g, ln2_b)
    return (y_prompt, y_sample)
```

```python
import math
from contextlib import ExitStack

import numpy as np
import concourse.bass as bass
import concourse.mybir as mybir
from concourse.bass_utils import run_bass_kernel_spmd

F32 = mybir.dt.float32
BF16 = mybir.dt.bfloat16
I32 = mybir.dt.int32
SLOT = 256
AF = mybir.ActivationFunctionType
ALU = mybir.AluOpType
AX = mybir.AxisListType

D = 1024
NCORES = 8
EPS = 1e-5
N_EXP = 32
DFF = 512


class _Op:
    __slots__ = ("eng", "fn", "r", "w", "dma", "sk")

    def __init__(self, eng, fn, r, w, dma, sk):
        self.eng, self.fn, self.r, self.w, self.dma, self.sk = eng, fn, r, w, dma, sk


class Sched:
    ROT = 60000

    def __init__(self, nc, stack):
        self.nc, self.stack = nc, stack
        self.engs = dict(pe=nc.tensor, act=nc.scalar, dve=nc.vector, pool=nc.gpsimd, sp=nc.sync)
        self.sems = {}
        self.allsems = {}
        self.nsem = 0
        self.known = {e: {} for e in self.engs}
        self.ops = []
        self.dmap = {}
        self.n_inst = 0

    def _newsem(self, key):
        sid = self.nsem
        self.nsem += 1
        h = self.stack.enter_context(self.nc.semaphore(f"sm{sid}"))
        ent = [h, 0, sid]
        self.sems[key] = ent
        self.allsems[sid] = ent
        return ent

    def op(self, eng, fn, r=(), w=(), dma=False, sk=None):
        self.ops.append(_Op(eng, fn, tuple(r), tuple(w), dma, sk))

    def mm(self, out, lhsT, rhs, start, stop, r, w):
        nc = self.nc
        self.op("pe", lambda: nc.tensor.matmul(out, lhsT=lhsT, rhs=rhs, start=start, stop=stop), r, w)

    def tr(self, out, in_, ident, r, w):
        nc = self.nc
        self.op("pe", lambda: nc.tensor.transpose(out, in_, ident), r, w)

    def act(self, out, in_, func, r, w, scale=1.0, bias=0.0, accum=None):
        nc = self.nc
        if accum is None:
            self.op("act", lambda: nc.scalar.activation(out=out, in_=in_, func=func, bias=bias, scale=scale), r, w)
        else:
            self.op("act", lambda: nc.scalar.activation(out=out, in_=in_, func=func, bias=bias, scale=scale,
                                                        accum_out=accum), r, w)

    def cp(self, eng, out, in_, r, w):
        nc = self.nc
        if eng == "act":
            self.op("act", lambda: nc.scalar.copy(out=out, in_=in_), r, w)
        else:
            e = self.engs[eng]
            self.op(eng, lambda: e.tensor_copy(out=out, in_=in_), r, w)

    def tt(self, eng, out, in0, in1, op, r, w):
        e = self.engs[eng]
        self.op(eng, lambda: e.tensor_tensor(out=out, in0=in0, in1=in1, op=op), r, w)

    def ts(self, eng, out, in0, s1, s2, op0, op1, r, w):
        e = self.engs[eng]
        if s2 is None:
            self.op(eng, lambda: e.tensor_scalar(out=out, in0=in0, scalar1=s1, scalar2=None, op0=op0), r, w)
        else:
            self.op(eng, lambda: e.tensor_scalar(out=out, in0=in0, scalar1=s1, scalar2=s2, op0=op0, op1=op1), r, w)

    def stt(self, eng, out, in0, scalar, in1, op0, op1, r, w):
        e = self.engs[eng]
        self.op(eng, lambda: e.scalar_tensor_tensor(out=out, in0=in0, scalar=scalar, in1=in1, op0=op0, op1=op1), r, w)

    def dma(self, q, out, in_, r, w, sk):
        e = self.engs[q]
        self.op(q, lambda: e.dma_start(out=out, in_=in_), r, w, dma=True, sk=sk)

    def flush(self):
        ops = self.ops
        n = len(ops)
        last_w, readers = {}, {}
        deps = [None] * n
        signal = [False] * n
        last_of_eng = {}
        for i, op in enumerate(ops):
            d_raw = set()
            d_oth = set()
            for k in op.r:
                j = last_w.get(k)
                if j is not None:
                    d_raw.add(j)
            for k in op.w:
                j = last_w.get(k)
                if j is not None:
                    d_oth.add(j)
                rd = readers.get(k)
                if rd:
                    d_oth.update(rd.values())
            d = set()
            for j in d_raw | d_oth:
                if j == i:
                    continue
                oj = ops[j]
                same = (not op.dma) and (not oj.dma) and oj.eng == op.eng
                if same:
                    if op.eng == "pe":
                        continue
                    if j not in d_raw:
                        continue
                d.add(j)
            deps[i] = sorted(d)
            for j in d:
                signal[j] = True
            for k in op.r:
                rk = ("d", i) if op.dma else op.eng
                readers.setdefault(k, {})[rk] = i
            for k in op.w:
                last_w[k] = i
                readers[k] = {}
            if not op.dma:
                last_of_eng[op.eng] = i
        for e, i in last_of_eng.items():
            signal[i] = True
        ev = [None] * n
        touched = set()
        for i, op in enumerate(ops):
            eng = self.engs[op.eng]
            kn = self.known[op.eng]
            for j in deps[i]:
                h, v, sid = ev[j]
                if kn.get(sid, 0) >= v:
                    continue
                eng.wait_ge(h, v)
                kn[sid] = v
                self.n_inst += 1
            ins = op.fn()
            self.n_inst += 1
            if op.dma or signal[i]:
                if op.dma:
                    if op.sk not in self.dmap:
                        self.dmap[op.sk] = len(self.dmap)
                    key = ("dma", self.dmap[op.sk])
                    inc = 16
                else:
                    key = ("eng", op.eng)
                    inc = 1
                ent = self.sems.get(key)
                if ent is None or ent[1] + inc > self.ROT:
                    ent = self._newsem(key)
                ent[1] += inc
                ins.then_inc(ent[0], inc)
                ev[i] = (ent[0], ent[1], ent[2])
                touched.add(ent[2])
        for ename, eng in self.engs.items():
            kn = self.known[ename]
            for sid in sorted(touched):
                h, cnt, _ = self.allsems[sid]
                if kn.get(sid, 0) < cnt:
                    eng.wait_ge(h, cnt)
                    kn[sid] = cnt
                    self.n_inst += 1
        self.ops = []
        self.dmap = {}


def host_constants(smax):
    ident = np.eye(128, dtype=np.float32)
    pos = np.arange(smax, dtype=np.float32)
    inv = (10000.0 ** (-np.arange(0, 128, 2, dtype=np.float32) / 128.0)).astype(np.float32)
    ang = pos[:, None] * inv[None, :]
    cs = np.concatenate([np.cos(ang), np.sin(ang)], axis=1).astype(np.float32)
    j = np.arange(128, dtype=np.float32)[:, None]
    i = np.arange(128, dtype=np.float32)[None, :]
    tri = np.stack([np.maximum(i - j, 0), np.maximum(j - i, 0),
                    (i >= j).astype(np.float32), (j > i).astype(np.float32)], axis=1)
    io = np.arange(128, dtype=np.float32)
    iota = np.stack([np.broadcast_to(io + 1.0, (128, 128)), np.broadcast_to(128.0 - io, (128, 128))], axis=1)
    pcol = np.stack([127.0 - io, io], axis=1)
    return dict(c_ident=ident, c_cs=cs, c_tri=np.ascontiguousarray(tri.astype(np.float32)),
                c_iota=np.ascontiguousarray(iota.astype(np.float32)),
                c_pcol=np.ascontiguousarray(pcol.astype(np.float32)))


class Builder:
    def __init__(self, s_list, depth, dbg=(), stop_after=None, alpha=None, use_moe=True, moe_stop=None):
        self.moe_stop = moe_stop
        self.alpha = float(alpha) if alpha is not None else (2.0 * depth) ** 0.25
        self.use_moe = use_moe
        self.s_list = list(s_list)
        self.depth = depth
        self.smax = max(s_list)
        self.dbg = set(dbg)
        self.stop_after = stop_after
        self.nc = bass.Bass("TRN2", target_bir_lowering=False)
        self.stack = ExitStack()
        self.S = Sched(self.nc, self.stack)
        self.uid = 0

    def din(self, name, shape, dt=F32):
        return self.nc.dram_tensor(name, list(shape), dt, kind="ExternalInput").ap()

    def dout(self, name, shape, dt=F32):
        return self.nc.dram_tensor(name, list(shape), dt, kind="ExternalOutput").ap()

    def dscr(self, name, shape, dt):
        kind = "ExternalOutput" if name in self.dbg else "Internal"
        return self.nc.dram_tensor(name, list(shape), dt, kind=kind).ap()

    def sb(self, st, name, shape, dt):
        self.uid += 1
        return st.enter_context(self.nc.sbuf_tensor(f"{name}_{self.uid}", list(shape), dt))

    def pst(self, st, name, shape, dt):
        self.uid += 1
        return st.enter_context(self.nc.psum_tensor(f"{name}_{self.uid}", list(shape), dt))

    def build(self):
        nc, S = self.nc, self.S
        L, smax = self.depth, self.smax
        self.x_in = [self.din(f"x{i}", [s, D]) for i, s in enumerate(self.s_list)]
        self.y_out = [self.dout(f"y{i}", [s, D]) for i, s in enumerate(self.s_list)]
        self.w_in = self.din("w_in", [L, D, 8192])
        self.ret_log_decay = self.din("ret_log_decay", [L, 8])
        self.ret_gn_gain = self.din("ret_gn_gain", [L, 1024])
        self.diff_lambda = self.din("diff_lambda", [L, 512])
        self.diff_subln_gain = self.din("diff_subln_gain", [L, 256])
        self.w_ret_branch = self.din("w_ret_branch", [L, 1024, 1024])
        self.w_diff_branch = self.din("w_diff_branch", [L, 1024, 1024])
        self.w_out = self.din("w_out", [L, 1024, 1024])
        self.ln1_g = self.din("ln1_g", [L, 1024])
        self.ln1_b = self.din("ln1_b", [L, 1024])
        self.router = self.din("router", [L, 1024, 36])
        if self.use_moe:
            self.w_gate_up = self.din("w_gate_up", [L, N_EXP, 1024, 1024])
            self.w_down = self.din("w_down", [L, N_EXP, 512, 1024])
        self.ln2_g = self.din("ln2_g", [L, 1024])
        self.ln2_b = self.din("ln2_b", [L, 1024])
        self.c_ident = self.din("c_ident", [128, 128])
        self.c_cs = self.din("c_cs", [smax, 128])
        self.c_tri = self.din("c_tri", [128, 4, 128])
        self.c_iota = self.din("c_iota", [128, 2, 128])
        self.c_pcol = self.din("c_pcol", [128, 2])
        self.QT = self.dscr("QT", [24, 128, smax], BF16)
        self.RKT = self.dscr("RKT", [smax, 512], BF16)
        self.RV = self.dscr("RV", [smax, 1024], BF16)
        self.SG = self.dscr("SG", [smax, 1024], BF16)
        self.DV = self.dscr("DV", [smax, 1024], BF16)
        self.GA = self.dscr("GA", [smax, 1024], BF16)
        self.GB = self.dscr("GB", [smax, 1024], BF16)
        self.BST = self.dscr("BST", [smax // 128, 128, 1024], BF16)
        self.RO = self.dscr("RO", [smax, 1024], BF16)
        self.DO = self.dscr("DO", [smax, 1024], BF16)
        self.stot = sum(self.s_list)
        self.X1 = self.dscr("X1", [self.stot, 1024], F32)
        self.XM = [self.dscr(f"XM{i}", [s, 1024], F32) for i, s in enumerate(self.s_list)]
        self.X1B = self.dscr("X1B", [self.stot, 1024], BF16)
        self.nslot_max = (2 * self.stot) // SLOT + 32
        self.XS = self.dscr("XS", [self.nslot_max * SLOT, 1024], BF16)
        self.YS = self.dscr("YS", [self.nslot_max * SLOT, 1024], F32)
        if self.use_moe:
            self.WGUB = self.dscr("WGUB", [N_EXP * 128, 8 * 1024], BF16)
            self.WDNB = self.dscr("WDNB", [N_EXP * 128, 4 * 1024], BF16)

        with self.stack:
            st = self.stack
            self.ident_f = st.enter_context(nc.sbuf_tensor("ident_f", [128, 128], F32))
            self.ident_b = st.enter_context(nc.sbuf_tensor("ident_b", [128, 128], BF16))
            S.dma("sp", self.ident_f[:], self.c_ident[:, :], [], ["ident_f"], "ident")
            S.cp("dve", self.ident_b[:], self.ident_f[:], ["ident_f"], ["ident_b"])
            self.mhalf = st.enter_context(nc.sbuf_tensor("mhalf", [128, 8], F32))
            S.op("pool", lambda: nc.gpsimd.memset(self.mhalf[:], -0.5), [], ["mhalf"])
            with ExitStack() as st0:
                zt = self.sb(st0, "zero_t", [128, 1024], BF16)
                S.op("dve", lambda: nc.vector.memset(zt[:], 0.0), [], ["zt"])
                for i in range(self.nslot_max * SLOT // 128):
                    S.dma("sp", self.XS[i * 128:(i + 1) * 128, :], zt[:], ["zt"], [("zchain", i % 4)], ("zinit", i % 4))
                S.flush()
            phases = ["p1a", "p2a", "p2b", "p3", "p4a"]
            done = False
            for l in range(L):
                for si, s in enumerate(self.s_list):
                    xsrc = self.x_in[si] if l == 0 else self.XM[si]
                    self.roff = sum(self.s_list[:si])
                    for ph in phases:
                        getattr(self, ph)(l, s, xsrc, None)
                        S.flush()
                        if self.stop_after == (l, si, ph):
                            done = True
                            break
                    if done:
                        break
                if done:
                    break
                xdsts = [self.y_out[si] if l == L - 1 else self.XM[si] for si in range(len(self.s_list))]
                self.p4b(l, xdsts)
                S.flush()
                if self.stop_after is not None and self.stop_after[0] == l and self.stop_after[2] == "p4b":
                    break
        return nc

    def p1a(self, l, s, xsrc, xdst):
        nc, S = self.nc, self.S
        nt = s // 128
        with ExitStack() as st:
            sb = lambda name, shape, dt: self.sb(st, name, shape, dt)
            wsbs = [sb(f"p1_w{i}", [128, 8, 4096], BF16) for i in range(2)]
            xs = [sb(f"p1_xs{i}", [128, 1024], F32) for i in range(2)]
            xT = [sb(f"p1_xT{i}", [128, 8, 128], BF16) for i in range(2)]
            ps = self.pst(st, "p1_ps", [128, 6, 4, 128], F32)
            cs = [sb(f"p1_cs{i}", [128, 128], F32) for i in range(2)]
            tmp = [sb(f"p1_tmp{i}", [128, 4, 4, 64], F32) for i in range(2)]
            qk = [sb(f"p1_qk{i}", [128, 24, 128], BF16) for i in range(2)]
            qTs = [sb(f"p1_qT{i}", [128, 24, 128], BF16) for i in range(2)]
            rvb = [sb(f"p1_rv{i}", [128, 1024], BF16) for i in range(2)]
            pq = self.pst(st, "p1_pq", [128, 2, 8, 128], BF16)
            sig = [sb(f"p1_sig{i}", [128, 512], F32) for i in range(2)]
            ob = {nm: [sb(f"p1_{nm}{i}", [128, 1024], BF16) for i in range(2)] for nm in ("sg", "dv", "ga", "gb")}
            wv = self.w_in[l].rearrange("(kc p) n -> p kc n", p=128)
            allgroups = [[(0, 2048, 0), (3072, 5120, 2048)], [(2048, 3072, 0), (5120, 8192, 1024)]]
            for sub_ in range(2):
                for gi, (c0, c1, o0) in enumerate(allgroups[sub_]):
                    for kc in range(8):
                        S.dma("pool", wsbs[sub_][:, kc, o0:o0 + (c1 - c0)], wv[:, kc, c0:c1], [], [("w", sub_, kc, gi)], ("w", sub_, kc, gi))
            for sub in range(2):
                self._p1_loop(l, s, xsrc, sub, nt, wsbs[sub], allgroups[sub], xs, xT, ps, cs, tmp, qk, qTs, rvb, pq, sig, ob)
            S.flush()

    def p1b(self, l, s, xsrc, xdst):
        pass

    def _p1_loop(self, l, s, xsrc, sub, nt, wsb, groups, xs, xT, ps, cs, tmp, qk, qTs, rvb, pq, sig, ob):
        nc, S = self.nc, self.S
        if True:
            if True:
                pass

            def wkeys(kc, cb):
                col = cb * 512
                gi = 0 if col < (groups[0][1] - groups[0][0]) else 1
                return ("w", sub, kc, gi)

            def load(t):
                sl = t % 2
                S.dma("sp", xs[sl][:], xsrc[t * 128:(t + 1) * 128, :], [], [("xs", sl)], ("xs", sl))
                if sub == 0:
                    S.dma("sp", cs[sl][:], self.c_cs[t * 128:(t + 1) * 128, :], [], [("cs", sl)], ("cs", sl))

            def compute(t):
                sl = t % 2
                for kc in range(8):
                    S.tr(ps[:, kc // 4, kc % 4, :], xs[sl][:, kc * 128:(kc + 1) * 128], self.ident_f[:],
                         [("xs", sl), "ident_f"], [("ps", kc // 4)])
                S.cp("act", xT[sl][:, 0:4, :], ps[:, 0], [("ps", 0)], [("xT", sl, 0)])
                S.cp("dve", xT[sl][:, 4:8, :], ps[:, 1], [("ps", 1)], [("xT", sl, 1)])
                for cb in range(8):
                    bank = 2 + cb % 4
                    pb = ps[:, bank].rearrange("p a b -> p (a b)")
                    for kc in range(8):
                        S.mm(pb, xT[sl][:, kc, :], wsb[:, kc, cb * 512:(cb + 1) * 512], kc == 0, kc == 7,
                             [("xT", sl, kc // 4), wkeys(kc, cb)], [("ps", bank)])
                    if sub == 0:
                        if cb in (2, 3):
                            S.cp("act", rvb[sl][:, (cb - 2) * 512:(cb - 1) * 512], pb, [("ps", bank)], [("rv", sl)])
                        else:
                            hb = {0: 0, 1: 4, 4: 8, 5: 12, 6: 16, 7: 20}[cb]
                            tsl = cb % 2
                            x1 = ps[:, bank, :, 0:64]
                            x2 = ps[:, bank, :, 64:128]
                            c = cs[sl][:, 0:64].unsqueeze(1).broadcast_to([128, 4, 64])
                            sn = cs[sl][:, 64:128].unsqueeze(1).broadcast_to([128, 4, 64])
                            tm = tmp[tsl]
                            S.tt("dve", tm[:, 0], x1, c, ALU.mult, [("ps", bank), ("cs", sl)], [("tmp", tsl, 0)])
                            S.tt("dve", tm[:, 1], x2, sn, ALU.mult, [("ps", bank), ("cs", sl)], [("tmp", tsl, 1)])
                            S.tt("dve", tm[:, 2], x2, c, ALU.mult, [("ps", bank), ("cs", sl)], [("tmp", tsl, 2)])
                            S.tt("dve", tm[:, 3], x1, sn, ALU.mult, [("ps", bank), ("cs", sl)], [("tmp", tsl, 3)])
                            S.tt("pool", qk[sl][:, hb:hb + 4, 0:64], tm[:, 0], tm[:, 1], ALU.subtract,
                                 [("tmp", tsl, 0), ("tmp", tsl, 1)], [("qk", sl, hb // 8)])
                            S.tt("pool", qk[sl][:, hb:hb + 4, 64:128], tm[:, 2], tm[:, 3], ALU.add,
                                 [("tmp", tsl, 2), ("tmp", tsl, 3)], [("qk", sl, hb // 8)])
                    else:
                        half = slice((cb % 2) * 512, (cb % 2 + 1) * 512)
                        if cb in (0, 1):
                            sg_t = sig[cb % 2]
                            S.act(sg_t[:], pb, AF.Sigmoid, [("ps", bank)], [("sig", cb % 2)])
                            S.tt("dve", ob["sg"][sl][:, half], sg_t[:], pb, ALU.mult,
                                 [("sig", cb % 2), ("ps", bank)], [("sg", sl)])
                        elif cb in (2, 3):
                            S.cp("act", ob["dv"][sl][:, half], pb, [("ps", bank)], [("dv", sl)])
                        elif cb in (4, 5):
                            S.act(ob["ga"][sl][:, half], pb, AF.Sigmoid, [("ps", bank)], [("ga", sl)])
                        else:
                            S.act(ob["gb"][sl][:, half], pb, AF.Sigmoid, [("ps", bank)], [("gb", sl)])
                if sub == 0:
                    for g8 in range(3):
                        pbk = g8 % 2
                        for gg in range(8):
                            g = g8 * 8 + gg
                            S.tr(pq[:, pbk, gg, :], qk[sl][:, g, :], self.ident_b[:],
                                 [("qk", sl, g // 8), "ident_b"], [("pq", pbk)])
                        S.cp("act" if g8 % 2 == 0 else "dve", qTs[sl][:, g8 * 8:(g8 + 1) * 8, :], pq[:, pbk],
                             [("pq", pbk)], [("qTs", sl, g8)])

            def store(t):
                sl = t % 2
                rows = slice(t * 128, (t + 1) * 128)
                if sub == 0:
                    S.dma("sp", self.QT[:, :, rows].rearrange("h d s -> d h s"), qTs[sl][:],
                          [("qTs", sl, 0), ("qTs", sl, 1), ("qTs", sl, 2)], [], ("st_q", sl))
                    S.dma("sp", self.RKT[rows, :].rearrange("p (h d) -> p h d", h=4), qk[sl][:, 4:8, :],
                          [("qk", sl, 0)], [], ("st_k", sl))
                    S.dma("sp", self.RV[rows, :], rvb[sl][:], [("rv", sl)], [], ("st_v", sl))
                else:
                    for nm, dst in (("sg", self.SG), ("dv", self.DV), ("ga", self.GA), ("gb", self.GB)):
                        S.dma("sp", dst[rows, :], ob[nm][sl][:], [(nm, sl)], [], ("st_" + nm, sl))

            load(0)
            for t in range(nt):
                if t + 1 < nt:
                    load(t + 1)
                compute(t)
                store(t)
            S.flush()

    def _p2_tables(self, l, st, full):
        nc, S = self.nc, self.S
        sb = lambda name, shape, dt: self.sb(st, name, shape, dt)
        T = {}
        ld = sb("p2_ld", [128, 8], F32)
        nl = sb("p2_nl", [128, 8], F32)
        pcol = sb("p2_pcol", [128, 2], F32)
        S.dma("sp", ld[:], self.ret_log_decay[l:l + 1, :].broadcast_to([128, 8]), [], ["ld"], "ld")
        S.dma("sp", pcol[:], self.c_pcol[:, :], [], ["pcol"], "pcol")
        S.ts("dve", nl[:], ld[:], -1.0, None, ALU.mult, None, ["ld"], ["nl0"])
        S.tt("dve", nl[:], nl[:], ld[:], ALU.min, ["nl0", "ld"], ["nl"])
        Z = sb("p2_Z", [128, 8], F32)
        DEC = sb("p2_DEC", [128, 8], F32)
        for h in range(4):
            S.act(Z[:, h:h + 1], pcol[:, 0:1], AF.Exp, ["pcol", "nl"], [("Zr", h)], scale=nl[:, h:h + 1])
            S.act(Z[:, 4 + h:5 + h], pcol[:, 1:2], AF.Exp, ["pcol", "nl"], [("Zr", 4 + h)], scale=nl[:, 4 + h:5 + h])
        S.ts("dve", Z[:], Z[:], 128.0 ** -0.5, None, ALU.mult, None, [("Zr", i) for i in range(8)], ["Z"])
        S.act(DEC[:], nl[:], AF.Exp, ["nl"], ["DEC"], scale=128.0)
        T.update(Z=Z, DEC=DEC)
        if full:
            tri = sb("p2_tri", [128, 4, 128], F32)
            iota = sb("p2_iota", [128, 2, 128], F32)
            S.dma("sp", tri[:], self.c_tri[:, :, :], [], ["tri"], "tri")
            S.dma("sp", iota[:], self.c_iota[:, :, :], [], ["iota"], "iota")
            MT = sb("p2_MT", [128, 4, 128], F32)
            XIF = sb("p2_XIF", [128, 4, 128], BF16)
            XIB = sb("p2_XIB", [128, 4, 128], BF16)
            ta = sb("p2_ta", [128, 128], F32)
            tb = sb("p2_tb", [128, 128], F32)
            for h in range(4):
                S.act(ta[:], tri[:, 0, :], AF.Exp, ["tri", "nl"], ["ta"], scale=nl[:, h:h + 1])
                S.tt("dve", ta[:], ta[:], tri[:, 2, :], ALU.mult, ["ta", "tri"], ["ta"])
                S.act(tb[:], tri[:, 1, :], AF.Exp, ["tri", "nl"], ["tb"], scale=nl[:, 4 + h:5 + h])
                S.tt("dve", tb[:], tb[:], tri[:, 3, :], ALU.mult, ["tb", "tri"], ["tb"])
                S.tt("dve", ta[:], ta[:], tb[:], ALU.add, ["ta", "tb"], ["ta"])
                S.ts("dve", MT[:, h, :], ta[:], 128.0 ** -0.5, None, ALU.mult, None, ["ta"], [("MT", h)])
                S.act(XIF[:, h, :], iota[:, 0, :], AF.Exp, ["iota", "nl"], [("XIF", h)], scale=nl[:, h:h + 1])
                S.act(XIB[:, h, :], iota[:, 1, :], AF.Exp, ["iota", "nl"], [("XIB", h)], scale=nl[:, 4 + h:5 + h])
            GN = sb("p2_GN", [128, 1024], F32)
            S.dma("sp", GN[:], self.ret_gn_gain[l:l + 1, :].broadcast_to([128, 1024]), [], ["GN"], "GN")
            T.update(MT=MT, XIF=XIF, XIB=XIB, GN=GN)
        return T

    def _state_update(self, st_f, st_bf_next, kt, v, kz, pst, zcol, dcol, T, sl, keys):
        S = self.S
        Z, DEC = T["Z"], T["DEC"]
        S.tt("dve", kz[:], kt.rearrange("p (h d) -> p h d", h=4),
             Z[:, zcol:zcol + 4].unsqueeze(2).broadcast_to([128, 4, 128]), ALU.mult,
             [keys["kt"], "Z"], [("kz", sl)])
        for h in range(4):
            S.mm(pst[:, h, :], kz[:, h, :], v[:, h * 256:(h + 1) * 256], True, True,
                 [("kz", sl), keys["v"]], [("pst", h // 2)])
        S.tt("pool", st_f[:], st_f[:], DEC[:, dcol:dcol + 4].unsqueeze(2).broadcast_to([128, 4, 256]), ALU.mult,
             ["stf", "DEC"], ["stf"])
        S.tt("dve", st_f[:], st_f[:], pst[:], ALU.add, ["stf", ("pst", 0), ("pst", 1)], ["stf"])
        S.cp("act", st_bf_next[0][:], st_f[:], ["stf"], [st_bf_next[1]])

    def p2a(self, l, s, xsrc, xdst):
        nc, S = self.nc, self.S
        nt = s // 128
        with ExitStack() as st:
            sb = lambda name, shape, dt: self.sb(st, name, shape, dt)
            T = self._p2_tables(l, st, False)
            kt = [sb(f"p2a_kt{i}", [128, 512], BF16) for i in range(2)]
            v = [sb(f"p2a_v{i}", [128, 1024], BF16) for i in range(2)]
            kz = [sb(f"p2a_kz{i}", [128, 4, 128], BF16) for i in range(2)]
            stf = sb("p2a_stf", [128, 4, 256], F32)
            stbf = [sb(f"p2a_stbf{i}", [128, 4, 256], BF16) for i in range(2)]
            pst = self.pst(st, "p2a_pst", [128, 4, 256], F32)
            S.op("dve", lambda: nc.vector.memset(stf[:], 0.0), [], ["stf"])
            S.op("dve", lambda: nc.vector.memset(stbf[0][:], 0.0), [], [("stbf", 0)])

            def load(c):
                sl = c % 2
                rows = slice(c * 128, (c + 1) * 128)
                S.dma("sp", kt[sl][:], self.RKT[rows, :], [], [("kt", sl)], ("kt", sl))
                S.dma("sp", v[sl][:], self.RV[rows, :], [], [("v", sl)], ("v", sl))

            order = list(range(nt - 1, -1, -1))
            if nt > 1:
                load(order[0])
            for idx, c in enumerate(order):
                cur = idx % 2
                S.dma("sp", self.BST[c], stbf[cur][:].rearrange("p h e -> p (h e)"), [("stbf", cur)], [], ("st_b", cur))
                if idx + 1 < nt:
                    if idx + 1 < nt - 1:
                        load(order[idx + 1])
                    sl = c % 2
                    self._state_update(stf, (stbf[1 - cur], ("stbf", 1 - cur)), kt[sl][:], v[sl], kz[sl], pst, 4, 4, T, sl,
                                       dict(kt=("kt", sl), v=("v", sl)))
            S.flush()

    def p2b(self, l, s, xsrc, xdst):
        nc, S = self.nc, self.S
        nt = s // 128
        with ExitStack() as st:
            sb = lambda name, shape, dt: self.sb(st, name, shape, dt)
            T = self._p2_tables(l, st, True)
            MT, XIF, XIB, GN = T["MT"], T["XIF"], T["XIB"], T["GN"]
            qT = [sb(f"p2_qT{i}", [128, 4, 128], BF16) for i in range(2)]
            kT = [sb(f"p2_kT{i}", [128, 4, 128], BF16) for i in range(2)]
            kt = [sb(f"p2_kt{i}", [128, 512], BF16) for i in range(2)]
            v = [sb(f"p2_v{i}", [128, 1024], BF16) for i in range(2)]
            sg = [sb(f"p2_sg{i}", [128, 1024], BF16) for i in range(2)]
            Bs = [sb(f"p2_B{i}", [128, 4, 256], BF16) for i in range(2)]
            qxf = [sb(f"p2_qxf{i}", [128, 4, 128], BF16) for i in range(2)]
            qxb = [sb(f"p2_qxb{i}", [128, 4, 128], BF16) for i in range(2)]
            pT = [sb(f"p2_pT{i}", [128, 4, 128], BF16) for i in range(2)]
            kz = [sb(f"p2_kz{i}", [128, 4, 128], BF16) for i in range(2)]
            on = [sb(f"p2_on{i}", [128, 1024], F32) for i in range(2)]
            ro = [sb(f"p2_ro{i}", [128, 1024], BF16) for i in range(2)]
            stats = sb("p2_stats", [128, 4, 6], F32)
            mv = sb("p2_mv", [128, 4, 2], F32)
            rstd = sb("p2_rstd", [128, 4], F32)
            nb = sb("p2_nb", [128, 4], F32)
            stf = sb("p2_stf", [128, 4, 256], F32)
            stbf = [sb(f"p2_stbf{i}", [128, 4, 256], BF16) for i in range(2)]
            pss = [self.pst(st, f"p2_pss{i}", [128, 4, 128], F32) for i in range(2)]
            po = [self.pst(st, f"p2_po{i}", [128, 4, 256], F32) for i in range(2)]
            pst = self.pst(st, "p2_pst", [128, 4, 256], F32)
            S.op("dve", lambda: nc.vector.memset(stf[:], 0.0), [], ["stf"])
            S.op("dve", lambda: nc.vector.memset(stbf[0][:], 0.0), [], [("stbf", 0)])

            def load(c):
                sl = c % 2
                rows = slice(c * 128, (c + 1) * 128)
                S.dma("sp", qT[sl][:], self.QT[0:4, :, rows].rearrange("h d s -> d h s"), [], [("qT", sl)], ("qT", sl))
                S.dma("sp", kT[sl][:], self.QT[4:8, :, rows].rearrange("h d s -> d h s"), [], [("kT", sl)], ("kT", sl))
                S.dma("sp", kt[sl][:], self.RKT[rows, :], [], [("kt", sl)], ("kt", sl))
                S.dma("sp", v[sl][:], self.RV[rows, :], [], [("v", sl)], ("v", sl))
                S.dma("sp", sg[sl][:], self.SG[rows, :], [], [("sg", sl)], ("sg", sl))
                S.dma("sp", Bs[sl][:].rearrange("p h e -> p (h e)"), self.BST[c], [], [("Bs", sl)], ("Bs", sl))

            def compute(c):
                sl = c % 2
                cur = c % 2
                S.tt("dve", qxf[sl][:], qT[sl][:], XIF[:], ALU.mult, [("qT", sl)] + [("XIF", h) for h in range(4)], [("qxf", sl)])
                S.tt("pool", qxb[sl][:], qT[sl][:], XIB[:], ALU.mult, [("qT", sl)] + [("XIB", h) for h in range(4)], [("qxb", sl)])
                for h in range(4):
                    S.mm(pss[sl][:, h, :], kT[sl][:, h, :], qT[sl][:, h, :], True, True, [("kT", sl), ("qT", sl)], [("pss", sl)])
                S.tt("dve", pT[sl][:], pss[sl][:], MT[:], ALU.mult, [("pss", sl)] + [("MT", h) for h in range(4)], [("pT", sl)])
                for h in range(4):
                    pk = ("po", sl, h // 2)
                    vv = v[sl][:, h * 256:(h + 1) * 256]
                    S.mm(po[sl][:, h, :], pT[sl][:, h, :], vv, True, False, [("pT", sl), ("v", sl)], [pk])
                    S.mm(po[sl][:, h, :], qxf[sl][:, h, :], stbf[cur][:, h, :], False, False, [("qxf", sl), ("stbf", cur)], [pk])
                    S.mm(po[sl][:, h, :], qxb[sl][:, h, :], Bs[sl][:, h, :], False, True, [("qxb", sl), ("Bs", sl)], [pk])
                for h in range(4):
                    S.op("dve", (lambda h=h: nc.vector.bn_stats(out=stats[:, h, :], in_=po[sl][:, h, :])),
                         [("po", sl, h // 2)], [("stats", h)])
                for h in range(4):
                    S.op("dve", (lambda h=h: nc.vector.bn_aggr(out=mv[:, h, :], in_=stats[:, h, :])), [("stats", h)], [("mv", h)])
                mvk = [("mv", h) for h in range(4)]
                S.ts("dve", rstd[:], mv[:, :, 1], EPS, None, ALU.add, None, mvk, ["rstd0"])
                S.tt("pool", rstd[:], rstd[:], self.mhalf[:, 0:4], ALU.pow, ["rstd0", "mhalf"], ["rstd"])
                S.stt("dve", nb[:], mv[:, :, 0], -1.0, rstd[:], ALU.mult, ALU.mult, mvk + ["rstd"], ["nb"])
                for h in range(4):
                    S.act(on[sl][:, h * 256:(h + 1) * 256], po[sl][:, h, :], AF.Identity, [("po", sl, h // 2), "rstd", "nb"],
                          [("on", sl, h)], scale=rstd[:, h:h + 1], bias=nb[:, h:h + 1])
                onk = [("on", sl, h) for h in range(4)]
                S.tt("dve", on[sl][:], on[sl][:], GN[:], ALU.mult, onk + ["GN"], [("on2", sl)])
                S.tt("dve", ro[sl][:], on[sl][:], sg[sl][:], ALU.mult, [("on2", sl), ("sg", sl)] + onk, [("ro", sl)])
                S.dma("sp", self.RO[c * 128:(c + 1) * 128, :], ro[sl][:], [("ro", sl)], [], ("st_ro", sl))
                if c + 1 < nt:
                    self._state_update(stf, (stbf[1 - cur], ("stbf", 1 - cur)), kt[sl][:], v[sl], kz[sl], pst, 0, 0, T, sl,
                                       dict(kt=("kt", sl), v=("v", sl)))

            load(0)
            for c in range(nt):
                if c + 1 < nt:
                    load(c + 1)
                compute(c)
            S.flush()

    def p3(self, l, s, xsrc, xdst):
        nc, S = self.nc, self.S
        nt = s // 128
        QB = 256
        lam_init = 0.8 - 0.6 * math.exp(-0.3 * l)
        with ExitStack() as st:
            sb = lambda name, shape, dt: self.sb(st, name, shape, dt)
            dl = sb("p3_dl", [128, 2, 2, 128], F32)
            prod = sb("p3_prod", [128, 2, 128], F32)
            sm = sb("p3_sm", [128, 2], F32)
            ee = sb("p3_ee", [128, 2], F32)
            nlam = sb("p3_nlam", [128, 1], F32)
            SUB = sb("p3_SUB", [128, 256], F32)
            S.dma("sp", dl[:].rearrange("p a b d -> p (a b d)"), self.diff_lambda[l:l + 1, :].broadcast_to([128, 512]), [], ["dl"], "dl")
            S.dma("sp", SUB[:], self.diff_subln_gain[l:l + 1, :].broadcast_to([128, 256]), [], ["SUBr"], "SUB")
            S.tt("dve", prod[:], dl[:, :, 0, :], dl[:, :, 1, :], ALU.mult, ["dl"], ["prod"])
            S.op("dve", lambda: nc.vector.reduce_sum(out=sm[:], in_=prod[:], axis=AX.X), ["prod"], ["sm"])
            S.act(ee[:], sm[:], AF.Exp, ["sm"], ["ee"])
            S.tt("dve", nlam[:], ee[:, 1:2], ee[:, 0:1], ALU.subtract, ["ee"], ["nlam0"])
            S.ts("dve", nlam[:], nlam[:], -lam_init, None, ALU.add, None, ["nlam0"], ["nlam"])
            S.ts("dve", SUB[:], SUB[:], 1.0 - lam_init, None, ALU.mult, None, ["SUBr"], ["SUB"])

            kTs = sb("p3_kT", [128, 2, s], BF16)
            Vs = sb("p3_V", [128, nt, 257], BF16)
            qT = [sb(f"p3_qT{i}", [128, 2, QB], BF16) for i in range(2)]
            pT = [sb(f"p3_pT{i}", [128, 2, 512], BF16) for i in range(2)]
            dsb = [sb(f"p3_do{i}", [128, 2, 256], BF16) for i in range(2)]
            accs = sb("p3_accs", [128, 4, 257], F32)
            rs = sb("p3_rs", [128, 2], F32)
            rs2 = sb("p3_rs2", [128, 1], F32)
            d1 = sb("p3_d1", [128, 256], F32)
            dd = sb("p3_dd", [128, 256], F32)
            junk = sb("p3_junk", [128, 256], F32)
            ss = sb("p3_ss", [128, 1], F32)
            rq = sb("p3_rq", [128, 1], F32)
            pss = [self.pst(st, f"p3_pss{i}", [128, 2, 512], F32) for i in range(2)]
            acc = self.pst(st, "p3_acc", [128, 4, 512], F32)
            S.op("pool", lambda: nc.gpsimd.memset(Vs[:, :, 256:257], 1.0), [], ["Vones"])
            if self.use_moe and s == self.s_list[0]:
                for e in range(N_EXP):
                    gsrc = self.w_gate_up[l, e].rearrange("(kc p) n -> p kc n", p=128)
                    gdst = self.WGUB[e * 128:(e + 1) * 128, :].rearrange("p (kc n) -> p kc n", kc=8)
                    for hf in range(2):
                        S.dma("pool", gdst[:, 4 * hf:4 * hf + 4, :], gsrc[:, 4 * hf:4 * hf + 4, :], [], [("pchain", hf)], ("precast", hf))
                    dsrc = self.w_down[l, e].rearrange("(j p) n -> p j n", p=128)
                    ddst = self.WDNB[e * 128:(e + 1) * 128, :].rearrange("p (j n) -> p j n", j=4)
                    S.dma("pool", ddst, dsrc, [], [("pchain", 2)], ("precast", 2))
            npair = nt // 2
            scale = 1.0 / math.sqrt(128.0)
            nqb = s // QB
            qcount = 0
            for h in range(4):
                for m in range(2):
                    S.dma("sp", kTs[:, m, :], self.QT[16 + 2 * h + m, :, 0:s], [], [("kT", m)], ("kT", m))
                S.dma("sp", Vs[:, :, 0:256], self.DV[0:s, h * 256:(h + 1) * 256].rearrange("(t p) c -> p t c", p=128),
                      [], ["V"], "V")

                def loadq(qb, sl, h=h):
                    S.dma("sp", qT[sl][:], self.QT[8 + 2 * h:8 + 2 * h + 2, :, qb * QB:(qb + 1) * QB].rearrange("m d s -> d m s"),
                          [], [("qT", sl)], ("qT", sl))

                loadq(0, qcount % 2)
                for qb in range(nqb):
                    sl = qcount % 2
                    if qb + 1 < nqb:
                        loadq(qb + 1, (qcount + 1) % 2)

                    def scores(j, sl=sl):
                        p = j % 2
                        for kk in range(2):
                            ktile = 2 * j + kk
                            for m in range(2):
                                S.mm(pss[p][:, kk, m * 256:(m + 1) * 256], kTs[:, m, ktile * 128:(ktile + 1) * 128], qT[sl][:, m, :],
                                     True, True, [("kT", m), ("qT", sl)], [("pss", p)])

                    def expo(j):
                        p = j % 2
                        S.act(pT[p][:], pss[p][:], AF.Exp, [("pss", p)], [("pT", p)], scale=scale)

                    def av(j):
                        p = j % 2
                        for kk in range(2):
                            ktile = 2 * j + kk
                            for m in range(2):
                                for qt in range(2):
                                    a = m * 2 + qt
                                    S.mm(acc[:, a, 0:257], pT[p][:, kk, m * 256 + qt * 128:m * 256 + (qt + 1) * 128], Vs[:, ktile, :],
                                         ktile == 0, ktile == nt - 1, [("pT", p), "V", "Vones"], [("acc", a)])

                    scores(0)
                    for j in range(npair):
                        if j + 1 < npair:
                            scores(j + 1)
                        expo(j)
                        av(j)
                    ds = dsb[sl]
                    S.cp("act", accs[:, 0:2, :], acc[:, 0:2, 0:257], [("acc", 0), ("acc", 1)], [("accs", 0), ("accs", 1)])
                    S.cp("dve", accs[:, 2:4, :], acc[:, 2:4, 0:257], [("acc", 2), ("acc", 3)], [("accs", 2), ("accs", 3)])
                    for qt in range(2):
                        a1 = accs[:, qt, :]
                        a2 = accs[:, 2 + qt, :]
                        S.op("dve", (lambda a1=a1: nc.vector.reciprocal(out=rs[:, 0:1], in_=a1[:, 256:257])), [("accs", qt)], [("rs", 0)])
                        S.op("dve", (lambda a2=a2: nc.vector.reciprocal(out=rs[:, 1:2], in_=a2[:, 256:257])), [("accs", 2 + qt)], [("rs", 1)])
                        S.tt("dve", rs2[:], rs[:, 1:2], nlam[:], ALU.mult, [("rs", 1), "nlam"], ["rs2"])
                        S.act(d1[:], a1[:, 0:256], AF.Identity, [("accs", qt), ("rs", 0)], ["d1"], scale=rs[:, 0:1])
                        S.stt("dve", dd[:], a2[:, 0:256], rs2[:, 0:1], d1[:], ALU.mult, ALU.add, [("accs", 2 + qt), "rs2", "d1"], ["dd"])
                        S.act(junk[:], dd[:], AF.Square, ["dd"], ["junk"], accum=ss[:])
                        S.ts("dve", rq[:], ss[:], 1.0 / 256.0, EPS, ALU.mult, ALU.add, ["junk"], ["rq0"])
                        S.tt("pool", rq[:], rq[:], self.mhalf[:, 0:1], ALU.pow, ["rq0", "mhalf"], ["rq"])
                        S.stt("dve", ds[:, qt, :], dd[:], rq[:, 0:1], SUB[:], ALU.mult, ALU.mult, ["dd", "rq", "SUB"], [("ds", sl, qt)])
                    S.dma("sp", self.DO[qb * QB:(qb + 1) * QB, h * 256:(h + 1) * 256].rearrange("(q p) c -> p q c", p=128), ds[:],
                          [("ds", sl, 0), ("ds", sl, 1)], [], ("st_do", sl))
                    qcount += 1
            S.flush()

    def _ln(self, eng_r, r, stats, mv, rstd, nb, xn, G, Bv, out, rk, outk, tail="pool"):
        nc, S = self.nc, self.S
        S.op("dve", lambda: nc.vector.bn_stats(out=stats[:, 0, :], in_=r[:, 0:512]), rk, [("lnst", 0)])
        S.op("dve", lambda: nc.vector.bn_stats(out=stats[:, 1, :], in_=r[:, 512:1024]), rk, [("lnst", 1)])
        S.op("dve", lambda: nc.vector.bn_aggr(out=mv[:], in_=stats[:].rearrange("p a b -> p (a b)")),
             [("lnst", 0), ("lnst", 1)], ["lnmv"])
        S.ts("dve", rstd[:], mv[:, 1:2], EPS, None, ALU.add, None, ["lnmv"], ["lnrstd0"])
        S.tt("pool", rstd[:], rstd[:], self.mhalf[:, 0:1], ALU.pow, ["lnrstd0", "mhalf"], ["lnrstd"])
        S.stt("dve", nb[:], mv[:, 0:1], -1.0, rstd[:], ALU.mult, ALU.mult, ["lnmv", "lnrstd"], ["lnnb"])
        if tail == "dve":
            S.ts("dve", xn[:], r[:], rstd[:, 0:1], nb[:, 0:1], ALU.mult, ALU.add, rk + ["lnrstd", "lnnb"], ["lnxn"])
        else:
            S.act(xn[:], r[:], AF.Identity, rk + ["lnrstd", "lnnb"], ["lnxn"], scale=rstd[:, 0:1], bias=nb[:, 0:1])
        S.tt(tail, xn[:], xn[:], G[:], ALU.mult, ["lnxn", "LNG"], ["lnxn2"])
        S.tt(tail, out, xn[:], Bv[:], ALU.add, ["lnxn2", "lnxn", "LNB"], outk)

    def p4a(self, l, s, xsrc, xdst):
        nc, S = self.nc, self.S
        nt = s // 128
        alpha = self.alpha
        with ExitStack() as st:
            sb = lambda name, shape, dt: self.sb(st, name, shape, dt)
            W = {}
            for nm, src in (("rb", self.w_ret_branch), ("db", self.w_diff_branch), ("wo", self.w_out)):
                W[nm] = sb("p4_w" + nm, [128, 8, 1024], BF16)
                wv = src[l].rearrange("(kc p) n -> p kc n", p=128)
                for kc in range(8):
                    S.dma("pool", W[nm][:, kc, :], wv[:, kc, :], [], [(nm, kc)], (nm, kc))
            G = sb("p4_G", [128, 1024], F32)
            Bv = sb("p4_B", [128, 1024], F32)
            S.dma("sp", G[:], self.ln1_g[l:l + 1, :].broadcast_to([128, 1024]), [], ["LNG"], "LNG")
            S.dma("sp", Bv[:], self.ln1_b[l:l + 1, :].broadcast_to([128, 1024]), [], ["LNB"], "LNB")
            inb = {nm: [sb(f"p4_{nm}{i}", [128, 1024], BF16) for i in range(2)] for nm in ("ro", "do", "ga", "gb")}
            xs = [sb(f"p4_xs{i}", [128, 1024], F32) for i in range(2)]
            roT = [sb(f"p4_roT{i}", [128, 8, 128], BF16) for i in range(2)]
            doT = [sb(f"p4_doT{i}", [128, 8, 128], BF16) for i in range(2)]
            mgT = [sb(f"p4_mgT{i}", [128, 8, 128], BF16) for i in range(2)]
            m1 = sb("p4_m1", [128, 1024], F32)
            m2 = sb("p4_m2", [128, 1024], F32)
            mg = sb("p4_mg", [128, 1024], BF16)
            r = sb("p4_r", [128, 1024], F32)
            xn = sb("p4_xn", [128, 1024], F32)
            x1s = [sb(f"p4_x1s{i}", [128, 1024], F32) for i in range(2)]
            x1b = [sb(f"p4_x1b{i}", [128, 1024], BF16) for i in range(2)]
            stats = sb("p4_stats", [128, 2, 6], F32)
            mv = sb("p4_mv", [128, 2], F32)
            rstd = sb("p4_rstd", [128, 1], F32)
            nb = sb("p4_nb", [128, 1], F32)
            pq = self.pst(st, "p4_pq", [128, 2, 8, 128], BF16)
            pA = self.pst(st, "p4_pA", [128, 2, 512], F32)
            pB = self.pst(st, "p4_pB", [128, 2, 512], F32)
            pM = self.pst(st, "p4_pM", [128, 2, 512], F32)
            srcs = dict(ro=self.RO, do=self.DO, ga=self.GA, gb=self.GB)

            def load(t):
                sl = t % 2
                rows = slice(t * 128, (t + 1) * 128)
                for nm in ("ro", "do", "ga", "gb"):
                    S.dma("sp", inb[nm][sl][:], srcs[nm][rows, :], [], [(nm, sl)], (nm, sl))
                S.dma("sp", xs[sl][:], xsrc[rows, :], [], [("xs", sl)], ("xs", sl))

            def compute(t):
                sl = t % 2
                for kc in range(8):
                    S.tr(pq[:, 0, kc, :], inb["ro"][sl][:, kc * 128:(kc + 1) * 128], self.ident_b[:], [("ro", sl), "ident_b"], [("pq", 0)])
                S.cp("act", roT[sl][:], pq[:, 0], [("pq", 0)], [("roT", sl)])
                for kc in range(8):
                    S.tr(pq[:, 1, kc, :], inb["do"][sl][:, kc * 128:(kc + 1) * 128], self.ident_b[:], [("do", sl), "ident_b"], [("pq", 1)])
                S.cp("dve", doT[sl][:], pq[:, 1], [("pq", 1)], [("doT", sl)])
                for cb in range(2):
                    for kc in range(8):
                        S.mm(pA[:, cb, :], roT[sl][:, kc, :], W["rb"][:, kc, cb * 512:(cb + 1) * 512], kc == 0, kc == 7,
                             [("roT", sl), ("rb", kc)], [("pA", cb)])
                for cb in range(2):
                    for kc in range(8):
                        S.mm(pB[:, cb, :], doT[sl][:, kc, :], W["db"][:, kc, cb * 512:(cb + 1) * 512], kc == 0, kc == 7,
                             [("doT", sl), ("db", kc)], [("pB", cb)])
                S.tt("dve", m1[:], inb["ga"][sl][:], pA[:].rearrange("p a b -> p (a b)"), ALU.mult,
                     [("ga", sl), ("pA", 0), ("pA", 1)], ["m1"])
                S.tt("dve", m2[:], inb["gb"][sl][:], pB[:].rearrange("p a b -> p (a b)"), ALU.mult,
                     [("gb", sl), ("pB", 0), ("pB", 1)], ["m2"])
                S.tt("dve", mg[:], m1[:], m2[:], ALU.add, ["m1", "m2"], ["mg"])
                for kc in range(8):
                    S.tr(pq[:, 0, kc, :], mg[:, kc * 128:(kc + 1) * 128], self.ident_b[:], ["mg", "ident_b"], [("pq", 0)])
                S.cp("act", mgT[sl][:], pq[:, 0], [("pq", 0)], [("mgT", sl)])
                for cb in range(2):
                    for kc in range(8):
                        S.mm(pM[:, cb, :], mgT[sl][:, kc, :], W["wo"][:, kc, cb * 512:(cb + 1) * 512], kc == 0, kc == 7,
                             [("mgT", sl), ("wo", kc)], [("pM", cb)])
                S.stt("dve", r[:], xs[sl][:], alpha, pM[:].rearrange("p a b -> p (a b)"), ALU.mult, ALU.add,
                      [("xs", sl), ("pM", 0), ("pM", 1)], ["r"])
                self._ln("dve", r, stats, mv, rstd, nb, xn, G, Bv, x1s[sl][:], ["r"], [("x1s", sl)], tail="dve")
                S.dma("sp", self.X1[self.roff + t * 128:self.roff + (t + 1) * 128, :], x1s[sl][:], [("x1s", sl)], [], ("st_x1", sl))
                S.cp("act", x1b[sl][:], x1s[sl][:], [("x1s", sl)], [("x1b", sl)])
                S.dma("sp", self.X1B[self.roff + t * 128:self.roff + (t + 1) * 128, :], x1b[sl][:], [("x1b", sl)], [], ("st_x1b", sl))

            load(0)
            for t in range(nt):
                if t + 1 < nt:
                    load(t + 1)
                compute(t)
            S.flush()

    def p4b(self, l, xdsts):
        nc, S = self.nc, self.S
        V = nc.vector
        alpha = self.alpha
        s = self.stot
        nt = s // 128
        nslot = (2 * s) // SLOT + 32
        J = max(s // SLOT, 1)
        tile_dst = []
        for si_q, sq in enumerate(self.s_list):
            for tq in range(sq // 128):
                tile_dst.append((xdsts[si_q], tq))
        with ExitStack() as st:
            sb = lambda name, shape, dt: self.sb(st, name, shape, dt)
            wr = sb("r_wr", [128, 8, 36], BF16)
            S.dma("pool", wr[:], self.router[l].rearrange("(kc p) n -> p kc n", p=128), [], ["wr"], "wr")
            G = sb("r_G", [128, 1024], F32)
            Bv = sb("r_B", [128, 1024], F32)
            S.dma("sp", G[:], self.ln2_g[l:l + 1, :].broadcast_to([128, 1024]), [], ["LNG"], "LNG")
            S.dma("sp", Bv[:], self.ln2_b[l:l + 1, :].broadcast_to([128, 1024]), [], ["LNB"], "LNB")
            tri = sb("r_tri", [128, 128], F32)
            iota = sb("r_iota", [128, 128], F32)
            S.dma("sp", tri[:], self.c_tri[:, 0, :], [], ["tri"], "tri")
            S.dma("sp", iota[:], self.c_iota[:, 0, :], [], ["iota"], "iota")
            U = sb("r_U", [128, 128], BF16)
            onesb = sb("r_ones", [128, 128], BF16)
            S.ts("dve", U[:], tri[:], 0.0, None, ALU.is_gt, None, ["tri"], ["U"])
            S.op("dve", lambda: V.memset(onesb[:], 1.0), [], ["onesb"])
            MK = sb("r_MK", [128, nt, 2, 32], F32)
            RK = sb("r_RK", [128, nt, 2], F32)
            WG = sb("r_WG", [128, nt, 2], F32)
            PI = sb("r_PI", [128, nt, 2], I32)
            Acum = sb("r_Acum", [128, 32], F32)
            AcumB = sb("r_AcumB", [128, 32], BF16)
            Ab = sb("r_Ab", [128, 32], BF16)
            eidi = sb("r_eidi", [128, 128], I32)
            pcol = sb("r_pcol", [128, 2], F32)
            S.dma("sp", pcol[:], self.c_pcol[:, :], [], ["pcol"], "pcol")
            base = sb("r_base", [128, 32], F32)
            S.op("dve", lambda: V.memset(Acum[:], 0.0), [], ["Acum"])
            S.op("dve", lambda: V.memset(AcumB[:], 0.0), [], ["AcumB"])
            xs = [sb(f"r_xs{i}", [128, 1024], F32) for i in range(2)]
            x1T = [sb(f"r_x1T{i}", [128, 8, 128], BF16) for i in range(2)]
            xb = [sb(f"r_xb{i}", [128, 1024], BF16) for i in range(2)]
            rl = sb("r_rl", [128, 36], F32)
            sm = {nm: sb("r_" + nm, [128, 1], F32) for nm in
                  ("gmax", "ngmax", "gsum", "gp", "m1", "m2", "dd", "ed", "den", "w1", "w2")}
            oh = sb("r_oh", [128, 4], F32)
            eg = sb("r_eg", [128, 4], F32)
            t3 = sb("r_t3", [128, 4, 8], F32)
            esel = sb("r_esel", [128, 8], F32)
            mk1 = sb("r_mk1", [128, 8], F32)
            mk2 = sb("r_mk2", [128, 8], F32)
            e2 = sb("r_e2", [128, 8], F32)
            t2 = sb("r_t2", [128, 2, 32], F32)
            posf = sb("r_posf", [128, 2], F32)
            big = sb("r_big", [128, 128 * 32], F32)
            ntot = sb("r_ntot", [128, 32], F32)
            nte = sb("r_nte", [128, 32], F32)
            thr = sb("r_thr", [128, 128], F32)
            cinc = sb("r_cinc", [128, 32], F32)
            eidf = sb("r_eidf", [128, 128], F32)
            gu = [sb(f"r_gu{i}", [128, 8, 1024], BF16) for i in range(2)]
            dn = [sb(f"r_dn{i}", [128, 4, 1024], BF16) for i in range(2)]
            xsl = [[sb(f"r_xsl{i}{q}", [128, 1024], BF16) for q in range(2)] for i in range(2)]
            xT = [sb(f"r_xT{i}", [128, 8, SLOT], BF16) for i in range(2)]
            actT = [sb(f"r_actT{i}", [128, 4, SLOT], BF16) for i in range(2)]
            slu = [sb(f"r_slu{i}", [128, SLOT], F32) for i in range(2)]
            yo = [sb(f"r_yo{i}", [128, 1024], F32) for i in range(2)]
            g1 = [sb(f"r_g1{i}", [128, 1024], F32) for i in range(2)]
            g2 = [sb(f"r_g2{i}", [128, 1024], F32) for i in range(2)]
            r = sb("r_r", [128, 1024], F32)
            xn = sb("r_xn", [128, 1024], F32)
            xo = [sb(f"r_xo{i}", [128, 1024], F32) for i in range(2)]
            stats = sb("r_stats", [128, 2, 6], F32)
            mv = sb("r_mv", [128, 2], F32)
            rstd = sb("r_rstd", [128, 1], F32)
            nb = sb("r_nb", [128, 1], F32)
            pq = self.pst(st, "r_pq", [128, 2, 8, 128], BF16)
            pgu = self.pst(st, "r_pgu", [128, 2, 512], F32)
            pdn = self.pst(st, "r_pdn", [128, 4, 512], F32)

            G_ = 8 if nt % 8 == 0 else (4 if nt % 4 == 0 else (2 if nt % 2 == 0 else 1))
            rlG = sb("r_rlG", [128, G_, 36], F32)
            f2 = {nm: sb("r_g_" + nm, [128, G_], F32) for nm in ("gmax", "gsum", "gp", "m1", "m2", "dd", "ed", "den", "w1", "w2")}
            ohG = sb("r_ohG", [128, G_, 4], F32)
            lgsG = sb("r_lgsG", [128, G_, 4], F32)
            egG = sb("r_egG", [128, G_, 4], F32)
            t3G = sb("r_t3G", [128, G_, 4, 8], F32)
            eselG = sb("r_eselG", [128, G_, 8], F32)
            mk1G = sb("r_mk1G", [128, G_, 8], F32)
            mk2G = sb("r_mk2G", [128, G_, 8], F32)
            e2G = sb("r_e2G", [128, G_, 8], F32)
            AbG = sb("r_AbG", [128, G_, 32], BF16)
            t2G = sb("r_t2G", [128, G_, 2, 32], F32)
            asum = sb("r_asum", [128, 32], F32)
            posG = sb("r_posG", [128, G_, 2], F32)
            b3 = lambda ap, n: ap.unsqueeze(2).broadcast_to([128, G_, n])
            S.dma("sp", xs[0][:], self.X1[0:128, :], [], [("xs", 0)], ("xs", 0))
            for g0 in range(0, nt, G_):
                for gi in range(G_):
                    t = g0 + gi
                    sl = t % 2
                    if t + 1 < nt:
                        S.dma("sp", xs[1 - sl][:], self.X1[(t + 1) * 128:(t + 2) * 128, :], [], [("xs", 1 - sl)], ("xs", 1 - sl))
                    for kc in range(8):
                        S.tr(pdn[:, kc // 4, (kc % 4) * 128:(kc % 4 + 1) * 128], xs[sl][:, kc * 128:(kc + 1) * 128], self.ident_f[:],
                             [("xs", sl), "ident_f"], [("pdn", kc // 4)])
                    S.cp("act", x1T[sl][:, 0:4, :], pdn[:, 0, :].rearrange("p (a b) -> p a b", a=4), [("pdn", 0)], [("x1T", sl, 0)])
                    S.cp("dve", x1T[sl][:, 4:8, :], pdn[:, 1, :].rearrange("p (a b) -> p a b", a=4), [("pdn", 1)], [("x1T", sl, 1)])
                    for kc in range(8):
                        S.mm(pdn[:, 2, gi * 36:(gi + 1) * 36], x1T[sl][:, kc, :], wr[:, kc, :], kc == 0, kc == 7,
                             [("x1T", sl, kc // 4), "wr"], [("pdn", 2)])
                S.cp("act", rlG[:].rearrange("p t n -> p (t n)"), pdn[:, 2, 0:G_ * 36], [("pdn", 2)], ["rl"])
                lg = rlG[:, :, 0:4]
                le = rlG[:, :, 4:36].rearrange("p t (g e) -> p t g e", g=4)
                S.op("dve", lambda lg=lg: V.reduce_max(out=f2["gmax"][:], in_=lg, axis=AX.X), ["rl"], ["gmax"])
                S.tt("dve", ohG[:], lg, b3(f2["gmax"][:], 4), ALU.is_equal, ["rl", "gmax"], ["oh"])
                S.tt("dve", lgsG[:], lg, b3(f2["gmax"][:], 4), ALU.subtract, ["rl", "gmax"], ["lgs"])
                S.act(egG[:], lgsG[:], AF.Exp, ["lgs"], ["eg"])
                S.op("dve", lambda: V.reduce_sum(out=f2["gsum"][:], in_=egG[:], axis=AX.X), ["eg"], ["gsum"])
                S.op("dve", lambda: V.reciprocal(out=f2["gp"][:], in_=f2["gsum"][:]), ["gsum"], ["gp"])
                S.tt("dve", t3G[:], le, ohG[:].unsqueeze(3).broadcast_to([128, G_, 4, 8]), ALU.mult, ["rl", "oh"], ["t3"])
                S.op("dve", lambda: V.tensor_reduce(out=eselG[:], in_=t3G[:].rearrange("p t g e -> p t e g"), axis=AX.X, op=ALU.add),
                     ["t3"], ["esel"])
                S.op("dve", lambda: V.reduce_max(out=f2["m1"][:], in_=eselG[:], axis=AX.X), ["esel"], ["m1"])
                S.tt("dve", mk1G[:], eselG[:], b3(f2["m1"][:], 8), ALU.is_equal, ["esel", "m1"], ["mk1"])
                S.stt("dve", e2G[:], mk1G[:], -1e30, eselG[:], ALU.mult, ALU.add, ["mk1", "esel"], ["e2"])
                S.op("dve", lambda: V.reduce_max(out=f2["m2"][:], in_=e2G[:], axis=AX.X), ["e2"], ["m2"])
                S.tt("dve", mk2G[:], e2G[:], b3(f2["m2"][:], 8), ALU.is_equal, ["e2", "m2"], ["mk2"])
                S.tt("dve", f2["dd"][:], f2["m2"][:], f2["m1"][:], ALU.subtract, ["m1", "m2"], ["dd"])
                S.act(f2["ed"][:], f2["dd"][:], AF.Exp, ["dd"], ["ed"])
                S.ts("dve", f2["den"][:], f2["ed"][:], 1.0, None, ALU.add, None, ["ed"], ["den"])
                S.op("dve", lambda: V.reciprocal(out=f2["w1"][:], in_=f2["den"][:]), ["den"], ["w1r"])
                S.tt("dve", f2["w2"][:], f2["ed"][:], f2["w1"][:], ALU.mult, ["ed", "w1r"], ["w2r"])
                S.tt("dve", WG[:, g0:g0 + G_, 0], f2["w1"][:], f2["gp"][:], ALU.mult, ["w1r", "gp"], [("WG", g0, 0)])
                S.tt("dve", WG[:, g0:g0 + G_, 1], f2["w2"][:], f2["gp"][:], ALU.mult, ["w2r", "gp"], [("WG", g0, 1)])
                ohb = ohG[:].unsqueeze(3).broadcast_to([128, G_, 4, 8])
                S.tt("dve", MK[:, g0:g0 + G_, 0, :].rearrange("p t (g e) -> p t g e", g=4), ohb,
                     mk1G[:].unsqueeze(2).broadcast_to([128, G_, 4, 8]), ALU.mult, ["oh", "mk1"], [("MK", g0, 0)])
                S.tt("dve", MK[:, g0:g0 + G_, 1, :].rearrange("p t (g e) -> p t g e", g=4), ohb,
                     mk2G[:].unsqueeze(2).broadcast_to([128, G_, 4, 8]), ALU.mult, ["oh", "mk2"], [("MK", g0, 1)])
                S.tt("dve", AbG[:], MK[:, g0:g0 + G_, 0, :], MK[:, g0:g0 + G_, 1, :], ALU.add, [("MK", g0, 0), ("MK", g0, 1)], ["Ab"])
                for gi in range(G_):
                    po_ = pdn[:, 3, gi * 32:(gi + 1) * 32]
                    S.mm(po_, U[:], AbG[:, gi, :], True, False, ["U", "Ab"], [("pdn", 3)])
                    S.mm(po_, onesb[:], AcumB[:], False, gi == 0, ["onesb", "AcumB"], [("pdn", 3)])
                    for gp_ in range(gi):
                        S.mm(po_, onesb[:], AbG[:, gp_, :], False, gp_ == gi - 1, ["onesb", "Ab"], [("pdn", 3)])
                pr2 = pdn[:, 3, 0:G_ * 32].rearrange("p (t e) -> p t e", e=32)
                S.tt("dve", t2G[:], MK[:, g0:g0 + G_, :, :], pr2.unsqueeze(2).broadcast_to([128, G_, 2, 32]), ALU.mult,
                     [("MK", g0, 0), ("MK", g0, 1), ("pdn", 3)], ["t2"])
                S.op("dve", (lambda g0=g0: V.reduce_sum(out=RK[:, g0:g0 + G_, :], in_=t2G[:], axis=AX.X)), ["t2"], [("RK", g0)])
                S.op("dve", lambda: V.tensor_reduce(out=asum[:], in_=AbG[:].rearrange("p t e -> p e t"), axis=AX.X, op=ALU.add),
                     ["Ab"], ["asum"])
                S.tt("dve", Acum[:], Acum[:], asum[:], ALU.add, ["Acum", "asum"], ["Acum"])
                S.cp("dve", AcumB[:], Acum[:], ["Acum"], ["AcumB"])
            S.mm(pdn[:, 3, 0:32], onesb[:], AcumB[:], True, True, ["onesb", "AcumB"], [("pdn", 3)])
            S.cp("act", ntot[:], pdn[:, 3, 0:32], [("pdn", 3)], ["ntot"])
            S.ts("dve", thr[:], iota[:], -1.0, float(SLOT), ALU.add, ALU.mult, ["iota"], ["thr"])
            bj = big[:, 0:32 * J].rearrange("p (e j) -> p e j", e=32)
            S.tt("dve", bj, ntot[:].unsqueeze(2).broadcast_to([128, 32, J]), thr[:, 0:J].unsqueeze(1).broadcast_to([128, 32, J]),
                 ALU.is_gt, ["ntot", "thr"], ["big"])
            S.op("dve", lambda: V.reduce_sum(out=nte[:], in_=bj, axis=AX.X), ["big"], ["nte"])
            be = big[:, 0:1024].rearrange("p (e f) -> p e f", e=32)
            S.tt("dve", be, iota[:, 0:32].unsqueeze(1).broadcast_to([128, 32, 32]), iota[:, 0:32].unsqueeze(2).broadcast_to([128, 32, 32]),
                 ALU.is_le, ["iota", "nte"], ["big2"])
            S.tt("dve", be, be, nte[:].unsqueeze(1).broadcast_to([128, 32, 32]), ALU.mult, ["big2", "nte"], ["big3"])
            S.op("dve", lambda: V.reduce_sum(out=cinc[:], in_=be, axis=AX.X), ["big3"], ["cinc"])
            S.tt("dve", base[:], cinc[:], nte[:], ALU.subtract, ["cinc", "nte"], ["base0"])
            S.ts("dve", base[:], base[:], float(SLOT), None, ALU.mult, None, ["base0"], ["base"])
            bs = big[:, 0:nslot * 32].rearrange("p (s e) -> p s e", e=32)
            S.tt("dve", bs, cinc[:].unsqueeze(1).broadcast_to([128, nslot, 32]), iota[:, 0:nslot].unsqueeze(2).broadcast_to([128, nslot, 32]),
                 ALU.is_lt, ["cinc", "iota", "big3"], ["big4"])
            S.op("dve", lambda: V.reduce_sum(out=eidf[:, 0:nslot], in_=bs, axis=AX.X), ["big4"], ["eidf0"])
            S.ts("dve", eidf[:, 0:nslot], eidf[:, 0:nslot], 31.0, None, ALU.min, None, ["eidf0"], ["eidf"])
            S.ts("dve", eidf[:, 0:nslot], eidf[:, 0:nslot], 128.0, pcol[:, 1:2], ALU.mult, ALU.add, ["eidf", "pcol"], ["eidf2"])
            S.cp("dve", eidi[:, 0:nslot], eidf[:, 0:nslot], ["eidf2"], ["eidi"])
            def dump():
                if "ROUTE" in self.dbg and not hasattr(self, "_dumped"):
                    self._dumped = True
                    d_pi = self.nc.dram_tensor("DBG_PI", [128, nt * 2], I32, kind="ExternalOutput").ap()
                    d_ei = self.nc.dram_tensor("DBG_EI", [128, 128], I32, kind="ExternalOutput").ap()
                    d_wg = self.nc.dram_tensor("DBG_WG", [128, nt * 2], F32, kind="ExternalOutput").ap()
                    d_nt = self.nc.dram_tensor("DBG_NT", [128, 32], F32, kind="ExternalOutput").ap()
                    d_rk = self.nc.dram_tensor("DBG_RK", [128, nt * 2], F32, kind="ExternalOutput").ap()
                    S.dma("sp", d_pi, PI[:].rearrange("p t k -> p (t k)"), [("PI", t_) for t_ in range(0, nt, G_)], [], "dbg0")
                    S.dma("sp", d_ei, eidi[:], ["eidi"], [], "dbg1")
                    S.dma("sp", d_wg, WG[:].rearrange("p t k -> p (t k)"), [], [], "dbg2")
                    S.dma("sp", d_nt, ntot[:], ["ntot"], [], "dbg3")
                    S.dma("sp", d_rk, RK[:].rearrange("p t k -> p (t k)"), [("RK", t_) for t_ in range(0, nt, G_)], [], "dbg4")
            for g0 in range(0, nt, G_):
                S.tt("dve", t2G[:], MK[:, g0:g0 + G_, :, :], base[:].unsqueeze(1).unsqueeze(1).broadcast_to([128, G_, 2, 32]), ALU.mult,
                     [("MK", g0, 0), ("MK", g0, 1), "base"], ["t2"])
                S.op("dve", lambda: V.reduce_sum(out=posG[:], in_=t2G[:], axis=AX.X), ["t2"], ["posf0"])
                S.tt("dve", posG[:], posG[:], RK[:, g0:g0 + G_, :], ALU.add, ["posf0", ("RK", g0)], ["posf"])
                S.cp("dve", PI[:, g0:g0 + G_, :], posG[:], ["posf"], [("PI", g0)])
            S.dma("sp", xb[0][:], self.X1B[0:128, :], [], [("xb", 0)], ("xb", 0))
            for t in range(nt):
                sl = t % 2
                if t + 1 < nt:
                    S.dma("sp", xb[1 - sl][:], self.X1B[(t + 1) * 128:(t + 2) * 128, :], [], [("xb", 1 - sl)], ("xb", 1 - sl))
                for k in range(2):
                    if self.moe_stop == "A0":
                        continue
                    S.op("pool", (lambda t=t, k=k, sl=sl: nc.gpsimd.indirect_dma_start(
                        out=self.XS[:, :], out_offset=bass.IndirectOffsetOnAxis(ap=PI[:, t, k:k + 1], axis=0),
                        in_=xb[sl][:], in_offset=None)), [("xb", sl), ("PI", (t // G_) * G_)], [], dma=True, sk=("scat", sl, k))
            dump()
            S.flush()
            if self.moe_stop in ("A0", "A"):
                return


            def load_slot(si_):
                es = si_ % 2
                S.op("pool", (lambda es=es, si_=si_: nc.gpsimd.indirect_dma_start(
                    out=gu[es][:].rearrange("p k n -> p (k n)"), out_offset=None, in_=self.WGUB[:, :],
                    in_offset=bass.IndirectOffsetOnAxis(ap=eidi[:, si_:si_ + 1], axis=0))), [], [("gu", es)], dma=True, sk=("gu", es))
                S.op("pool", (lambda es=es, si_=si_: nc.gpsimd.indirect_dma_start(
                    out=dn[es][:].rearrange("p k n -> p (k n)"), out_offset=None, in_=self.WDNB[:, :],
                    in_offset=bass.IndirectOffsetOnAxis(ap=eidi[:, si_:si_ + 1], axis=0))), [], [("dn", es)], dma=True, sk=("dn", es))
                for q in range(2):
                    S.dma("sp", xsl[es][q][:], self.XS[si_ * SLOT + q * 128:si_ * SLOT + (q + 1) * 128, :], [], [("xsl", es, q)], ("xsl", es, q))

            load_slot(0)
            for si_ in range(nslot):
                es = si_ % 2
                if si_ + 1 < nslot:
                    load_slot(si_ + 1)
                for q in range(2):
                    for kc in range(8):
                        S.tr(pq[:, q, kc, :], xsl[es][q][:, kc * 128:(kc + 1) * 128], self.ident_b[:], [("xsl", es, q), "ident_b"], [("pq", q)])
                    S.cp("act" if q == 0 else "dve", xT[es][:, :, q * 128:(q + 1) * 128], pq[:, q], [("pq", q)], [("xT", es, q)])
                for j in range(4):
                    pp = j % 2
                    pa = pgu[:, pp, 0:SLOT]
                    pg = pgu[:, pp, SLOT:2 * SLOT]
                    for kc in range(8):
                        S.mm(pa, gu[es][:, kc, j * 128:(j + 1) * 128], xT[es][:, kc, :], kc == 0, kc == 7,
                             [("gu", es), ("xT", es, 0), ("xT", es, 1)], [("pgu", pp)])
                    for kc in range(8):
                        S.mm(pg, gu[es][:, kc, 512 + j * 128:512 + (j + 1) * 128], xT[es][:, kc, :], kc == 0, kc == 7,
                             [("gu", es), ("xT", es, 0), ("xT", es, 1)], [("pgu", pp)])
                    S.act(slu[pp][:], pa, AF.Silu, [("pgu", pp)], [("slu", pp)])
                    S.tt("dve", actT[es][:, j, :], slu[pp][:], pg, ALU.mult, [("slu", pp), ("pgu", pp)], [("actT", es, j)])
                for q in range(2):
                    for cb in range(2):
                        for j in range(4):
                            S.mm(pdn[:, 2 * q + cb, :], actT[es][:, j, q * 128:(q + 1) * 128], dn[es][:, j, cb * 512:(cb + 1) * 512],
                                 j == 0, j == 3, [("actT", es, j), ("dn", es)], [("pdn", 2 * q + cb)])
                    S.cp("act", yo[q][:], pdn[:, 2 * q:2 * q + 2, :].rearrange("p a b -> p (a b)"), [("pdn", 2 * q), ("pdn", 2 * q + 1)], [("yo", q)])
                    S.dma("sp", self.YS[si_ * SLOT + q * 128:si_ * SLOT + (q + 1) * 128, :], yo[q][:], [("yo", q)], [], ("st_yo", q))
            S.flush()
            if self.moe_stop == "B":
                return

            def load_c(t):
                sl = t % 2
                S.dma("sp", xs[sl][:], self.X1[t * 128:(t + 1) * 128, :], [], [("xs", sl)], ("xs", sl))
                for k, gb_ in enumerate((g1, g2)):
                    S.op("pool", (lambda t=t, k=k, sl=sl, gb_=gb_: nc.gpsimd.indirect_dma_start(
                        out=gb_[sl][:], out_offset=None, in_=self.YS[:, :],
                        in_offset=bass.IndirectOffsetOnAxis(ap=PI[:, t, k:k + 1], axis=0))), [], [("g", k, sl)], dma=True, sk=("gath", sl, k))

            load_c(0)
            for t in range(nt):
                sl = t % 2
                if t + 1 < nt:
                    load_c(t + 1)
                S.ts("dve", g1[sl][:], g1[sl][:], WG[:, t, 0:1], None, ALU.mult, None, [("g", 0, sl)], [("g", 0, sl)])
                S.stt("dve", g2[sl][:], g2[sl][:], WG[:, t, 1:2], g1[sl][:], ALU.mult, ALU.add, [("g", 1, sl), ("g", 0, sl)], [("g", 1, sl)])
                S.stt("dve", r[:], xs[sl][:], alpha, g2[sl][:], ALU.mult, ALU.add, [("xs", sl), ("g", 1, sl)], ["r"])
                self._ln("dve", r, stats, mv, rstd, nb, xn, G, Bv, xo[sl][:], ["r"], [("xo", sl)])
                xd_, tq_ = tile_dst[t]
                S.dma("sp", xd_[tq_ * 128:(tq_ + 1) * 128, :], xo[sl][:], [("xo", sl)], [], ("st_xo", sl))
            S.flush()


    def p4b_dense(self, l, s, xsrc, xdst):
        nc, S = self.nc, self.S
        alpha = self.alpha
        SB = min(s, 2048)
        BK = min(SB, 512)
        ntb = SB // 128
        nblk = SB // BK
        tpb = BK // 128
        with ExitStack() as st:
            sb = lambda name, shape, dt: self.sb(st, name, shape, dt)
            wr = sb("p5_wr", [128, 8, 36], BF16)
            S.dma("pool", wr[:], self.router[l].rearrange("(kc p) n -> p kc n", p=128), [], ["wr"], "wr")
            G = sb("p5_G", [128, 1024], F32)
            Bv = sb("p5_B", [128, 1024], F32)
            S.dma("sp", G[:], self.ln2_g[l:l + 1, :].broadcast_to([128, 1024]), [], ["LNG"], "LNG")
            S.dma("sp", Bv[:], self.ln2_b[l:l + 1, :].broadcast_to([128, 1024]), [], ["LNB"], "LNB")
            x1T = sb("p5_x1T", [128, 8, SB], BF16)
            yacc = sb("p5_yacc", [128, ntb, 1024], F32)
            gu = [sb(f"p5_gu{i}", [128, 8, 1024], BF16) for i in range(2)]
            dn = [sb(f"p5_dn{i}", [128, 4, 1024], BF16) for i in range(2)]
            actT = [sb(f"p5_actT{i}", [128, 4, BK], BF16) for i in range(2)]
            slu = [sb(f"p5_slu{i}", [128, BK], F32) for i in range(2)]
            C = sb("p5_C", [128, ntb, 32], F32)
            xs = [sb(f"p5_xs{i}", [128, 1024], F32) for i in range(2)]
            r = sb("p5_r", [128, 1024], F32)
            xn = sb("p5_xn", [128, 1024], F32)
            xo = [sb(f"p5_xo{i}", [128, 1024], F32) for i in range(2)]
            stats = sb("p5_stats", [128, 2, 6], F32)
            mv = sb("p5_mv", [128, 2], F32)
            rstd = sb("p5_rstd", [128, 1], F32)
            nb = sb("p5_nb", [128, 1], F32)
            rl = sb("p5_rl", [128, 36], F32)
            sm = {nm: sb("p5_" + nm, [128, 1], F32) for nm in
                  ("gmax", "ngmax", "gsum", "gp", "m1", "m2", "dd", "ed", "den", "w1", "w2")}
            oh = sb("p5_oh", [128, 4], F32)
            eg = sb("p5_eg", [128, 4], F32)
            t3 = sb("p5_t3", [128, 4, 8], F32)
            esel = sb("p5_esel", [128, 8], F32)
            mk1 = sb("p5_mk1", [128, 8], F32)
            mk2 = sb("p5_mk2", [128, 8], F32)
            e2 = sb("p5_e2", [128, 8], F32)
            cl = sb("p5_cl", [128, 8], F32)
            pgu = self.pst(st, "p5_pgu", [128, 4, 512], F32)
            pdn = self.pst(st, "p5_pdn", [128, 4, 512], F32)
            ecount = 0

            def load_w(e, es):
                guv = self.w_gate_up[l, e].rearrange("(kc p) n -> p kc n", p=128)
                for q4 in range(4):
                    S.dma("pool", gu[es][:, 2 * q4:2 * q4 + 2, :], guv[:, 2 * q4:2 * q4 + 2, :], [], [("gu", es, q4)], ("gu", es, q4))
                S.dma("pool", dn[es][:], self.w_down[l, e].rearrange("(j p) n -> p j n", p=128), [], [("dn", es)], ("dn", es))

            for sbi in range(s // SB):
                base = sbi * SB
                load_w(0, ecount % 2)
                S.dma("sp", xs[0][:], self.X1[base:base + 128, :], [], [("xs", 0)], ("xs", 0))
                for tl in range(ntb):
                    sl = tl % 2
                    if tl + 1 < ntb:
                        S.dma("sp", xs[1 - sl][:], self.X1[base + (tl + 1) * 128:base + (tl + 2) * 128, :], [], [("xs", 1 - sl)], ("xs", 1 - sl))
                    for kc in range(8):
                        S.tr(pgu[:, kc // 4, (kc % 4) * 128:(kc % 4 + 1) * 128], xs[sl][:, kc * 128:(kc + 1) * 128], self.ident_f[:],
                             [("xs", sl), "ident_f"], [("pgu", kc // 4)])
                    cols = slice(tl * 128, (tl + 1) * 128)
                    S.cp("act", x1T[:, 0:4, cols], pgu[:, 0, :].rearrange("p (a b) -> p a b", a=4), [("pgu", 0)], [("x1T", tl, 0)])
                    S.cp("dve", x1T[:, 4:8, cols], pgu[:, 1, :].rearrange("p (a b) -> p a b", a=4), [("pgu", 1)], [("x1T", tl, 1)])
                    for kc in range(8):
                        S.mm(pdn[:, 0, 0:36], x1T[:, kc, cols], wr[:, kc, :], kc == 0, kc == 7, [("x1T", tl, kc // 4), "wr"], [("pdn", 0)])
                    S.cp("act", rl[:], pdn[:, 0, 0:36], [("pdn", 0)], ["rl"])
                    V = nc.vector
                    S.op("dve", lambda: V.reduce_max(out=sm["gmax"][:], in_=rl[:, 0:4], axis=AX.X), ["rl"], ["gmax"])
                    S.ts("dve", oh[:], rl[:, 0:4], sm["gmax"][:, 0:1], None, ALU.is_equal, None, ["rl", "gmax"], ["oh"])
                    S.ts("dve", sm["ngmax"][:], sm["gmax"][:], -1.0, None, ALU.mult, None, ["gmax"], ["ngmax"])
                    S.act(eg[:], rl[:, 0:4], AF.Exp, ["rl", "ngmax"], ["eg"], bias=sm["ngmax"][:, 0:1], accum=sm["gsum"][:])
                    S.op("dve", lambda: V.reciprocal(out=sm["gp"][:], in_=sm["gsum"][:]), ["eg"], ["gp"])
                    S.tt("dve", t3[:], rl[:, 4:36].rearrange("p (g e) -> p g e", g=4), oh[:].unsqueeze(2).broadcast_to([128, 4, 8]),
                         ALU.mult, ["rl", "oh"], ["t3"])
                    S.op("dve", lambda: V.tensor_reduce(out=esel[:], in_=t3[:].rearrange("p g e -> p e g"), axis=AX.X, op=ALU.add),
                         ["t3"], ["esel"])
                    S.op("dve", lambda: V.reduce_max(out=sm["m1"][:], in_=esel[:], axis=AX.X), ["esel"], ["m1"])
                    S.ts("dve", mk1[:], esel[:], sm["m1"][:, 0:1], None, ALU.is_equal, None, ["esel", "m1"], ["mk1"])
                    S.stt("dve", e2[:], mk1[:], -1e30, esel[:], ALU.mult, ALU.add, ["mk1", "esel"], ["e2"])
                    S.op("dve", lambda: V.reduce_max(out=sm["m2"][:], in_=e2[:], axis=AX.X), ["e2"], ["m2"])
                    S.ts("dve", mk2[:], e2[:], sm["m2"][:, 0:1], None, ALU.is_equal, None, ["e2", "m2"], ["mk2"])
                    S.tt("dve", sm["dd"][:], sm["m2"][:], sm["m1"][:], ALU.subtract, ["m1", "m2"], ["dd"])
                    S.act(sm["ed"][:], sm["dd"][:], AF.Exp, ["dd"], ["ed"])
                    S.ts("dve", sm["den"][:], sm["ed"][:], 1.0, None, ALU.add, None, ["ed"], ["den"])
                    S.op("dve", lambda: V.reciprocal(out=sm["w1"][:], in_=sm["den"][:]), ["den"], ["w1r"])
                    S.tt("dve", sm["w2"][:], sm["ed"][:], sm["w1"][:], ALU.mult, ["ed", "w1r"], ["w2r"])
                    S.tt("dve", sm["w1"][:], sm["w1"][:], sm["gp"][:], ALU.mult, ["w1r", "gp", "w2r"], ["w1"])
                    S.tt("dve", sm["w2"][:], sm["w2"][:], sm["gp"][:], ALU.mult, ["w2r", "gp"], ["w2"])
                    S.ts("dve", cl[:], mk1[:], sm["w1"][:, 0:1], None, ALU.mult, None, ["mk1", "w1"], ["cl0"])
                    S.stt("dve", cl[:], mk2[:], sm["w2"][:, 0:1], cl[:], ALU.mult, ALU.add, ["mk2", "w2", "cl0"], ["cl"])
                    S.tt("dve", C[:, tl, :].rearrange("p (g e) -> p g e", g=4), oh[:].unsqueeze(2).broadcast_to([128, 4, 8]),
                         cl[:].unsqueeze(1).broadcast_to([128, 4, 8]), ALU.mult, ["oh", "cl"], [("C", tl)])
                for e in range(N_EXP):
                    es = ecount % 2
                    if e + 1 < N_EXP:
                        load_w(e + 1, (ecount + 1) % 2)
                    for blk in range(nblk):
                        a_s = blk % 2
                        bcols = slice(blk * BK, (blk + 1) * BK)
                        for j in range(4):
                            pp = j % 2
                            pa = pgu[:, 2 * pp, 0:BK]
                            pg = pgu[:, 2 * pp + 1, 0:BK]
                            for kc in range(8):
                                S.mm(pa, gu[es][:, kc, j * 128:(j + 1) * 128], x1T[:, kc, bcols], kc == 0, kc == 7,
                                     [("gu", es, kc // 2)] + [("x1T", blk * tpb + q, kc // 4) for q in range(tpb)], [("pgu", 2 * pp)])
                            for kc in range(8):
                                S.mm(pg, gu[es][:, kc, 512 + j * 128:512 + (j + 1) * 128], x1T[:, kc, bcols], kc == 0, kc == 7,
                                     [("gu", es, kc // 2)] + [("x1T", blk * tpb + q, kc // 4) for q in range(tpb)], [("pgu", 2 * pp + 1)])
                            S.act(slu[pp][:], pa, AF.Silu, [("pgu", 2 * pp)], [("slu", pp)])
                            S.tt("dve", actT[a_s][:, j, :], slu[pp][:], pg, ALU.mult, [("slu", pp), ("pgu", 2 * pp + 1)], [("actT", a_s, j)])
                        for q in range(tpb):
                            tl = blk * tpb + q
                            dp = q % 2
                            for cb in range(2):
                                for j in range(4):
                                    S.mm(pdn[:, 2 * dp + cb, :], actT[a_s][:, j, q * 128:(q + 1) * 128], dn[es][:, j, cb * 512:(cb + 1) * 512],
                                         j == 0, j == 3, [("actT", a_s, j), ("dn", es)], [("pdn", 2 * dp + cb)])
                            pv = pdn[:, 2 * dp:2 * dp + 2, :].rearrange("p a b -> p (a b)")
                            pk = [("pdn", 2 * dp), ("pdn", 2 * dp + 1)]
                            if e == 0:
                                S.ts("dve", yacc[:, tl, :], pv, C[:, tl, e:e + 1], None, ALU.mult, None, pk + [("C", tl)], [("yacc", tl)])
                            else:
                                S.stt("dve", yacc[:, tl, :], pv, C[:, tl, e:e + 1], yacc[:, tl, :], ALU.mult, ALU.add,
                                      pk + [("C", tl), ("yacc", tl)], [("yacc", tl)])
                    ecount += 1
                S.dma("sp", xs[0][:], self.X1[base:base + 128, :], [], [("xs", 0)], ("xs", 0))
                for tl in range(ntb):
                    sl = tl % 2
                    if tl + 1 < ntb:
                        S.dma("sp", xs[1 - sl][:], self.X1[base + (tl + 1) * 128:base + (tl + 2) * 128, :], [], [("xs", 1 - sl)], ("xs", 1 - sl))
                    S.stt("dve", r[:], xs[sl][:], alpha, yacc[:, tl, :], ALU.mult, ALU.add, [("xs", sl), ("yacc", tl)], ["r"])
                    self._ln("dve", r, stats, mv, rstd, nb, xn, G, Bv, xo[sl][:], ["r"], [("xo", sl)])
                    S.dma("sp", xdst[base + tl * 128:base + (tl + 1) * 128, :], xo[sl][:], [("xo", sl)], [], ("st_xo", sl))
            S.flush()


def prep_inputs(inputs, s_list, depth, names=("x_prompt", "x_sample")):
    L = depth
    f = lambda a: np.ascontiguousarray(np.asarray(a, dtype=np.float32))
    shared = dict(
        w_in=f(inputs["w_in"][:L]),
        ret_log_decay=f(inputs["ret_log_decay"][:L]).reshape(L, 8),
        ret_gn_gain=f(inputs["ret_gn_gain"][:L]),
        diff_lambda=f(inputs["diff_lambda"][:L]).reshape(L, 512),
        diff_subln_gain=f(inputs["diff_subln_gain"][:L]),
        w_ret_branch=f(inputs["w_ret_branch"][:L]),
        w_diff_branch=f(inputs["w_diff_branch"][:L]),
        w_out=f(inputs["w_out"][:L]),
        ln1_g=f(inputs["ln1_g"][:L]), ln1_b=f(inputs["ln1_b"][:L]),
        router=np.ascontiguousarray(np.concatenate([f(inputs["router_group"][:L]), f(inputs["router_expert"][:L])], axis=-1)),
        w_gate_up=f(inputs["w_gate_up"][:L]), w_down=f(inputs["w_down"][:L]),
        ln2_g=f(inputs["ln2_g"][:L]), ln2_b=f(inputs["ln2_b"][:L]),
    )
    shared.update(host_constants(max(s_list)))
    maps = []
    for c in range(NCORES):
        m = dict(shared)
        for i, nm in enumerate(names):
            m[f"x{i}"] = f(inputs[nm][c])
        maps.append(m)
    return maps


def kernel(**inputs):
    s_list = [inputs["x_prompt"].shape[1], inputs["x_sample"].shape[1]]
    depth = inputs["w_in"].shape[0]
    b = Builder(s_list, depth)
    nc = b.build()
    maps = prep_inputs(inputs, s_list, depth)
    res = run_bass_kernel_spmd(nc, maps, core_ids=list(range(NCORES)))
    outs = []
    for i in range(2):
        outs.append(np.stack([np.asarray(res.results[c][f"y{i}"], dtype=np.float32) for c in range(NCORES)], axis=0))
    return tuple(outs)
```

```python
import math
from contextlib import ExitStack

import numpy as np
import concourse.bass as bass
import concourse.mybir as mybir
from concourse.bass_utils import run_bass_kernel_spmd

F32 = mybir.dt.float32
BF16 = mybir.dt.bfloat16
I32 = mybir.dt.int32
SLOT = 256
AF = mybir.ActivationFunctionType
ALU = mybir.AluOpType
AX = mybir.AxisListType

D = 1024
NCORES = 8
EPS = 1e-5
N_EXP = 32
DFF = 512


class _Op:
    __slots__ = ("eng", "fn", "r", "w", "dma", "sk")

    def __init__(self, eng, fn, r, w, dma, sk):
        self.eng, self.fn, self.r, self.w, self.dma, self.sk = eng, fn, r, w, dma, sk


class Sched:
    ROT = 60000

    def __init__(self, nc, stack):
        self.nc, self.stack = nc, stack
        self.engs = dict(pe=nc.tensor, act=nc.scalar, dve=nc.vector, pool=nc.gpsimd, sp=nc.sync)
        self.sems = {}
        self.allsems = {}
        self.nsem = 0
        self.known = {e: {} for e in self.engs}
        self.ops = []
        self.dmap = {}
        self.n_inst = 0

    def _newsem(self, key):
        sid = self.nsem
        self.nsem += 1
        h = self.stack.enter_context(self.nc.semaphore(f"sm{sid}"))
        ent = [h, 0, sid]
        self.sems[key] = ent
        self.allsems[sid] = ent
        return ent

    def op(self, eng, fn, r=(), w=(), dma=False, sk=None):
        self.ops.append(_Op(eng, fn, tuple(r), tuple(w), dma, sk))

    def mm(self, out, lhsT, rhs, start, stop, r, w):
        nc = self.nc
        self.op("pe", lambda: nc.tensor.matmul(out, lhsT=lhsT, rhs=rhs, start=start, stop=stop), r, w)

    def tr(self, out, in_, ident, r, w):
        nc = self.nc
        self.op("pe", lambda: nc.tensor.transpose(out, in_, ident), r, w)

    def act(self, out, in_, func, r, w, scale=1.0, bias=0.0, accum=None):
        nc = self.nc
        if accum is None:
            self.op("act", lambda: nc.scalar.activation(out=out, in_=in_, func=func, bias=bias, scale=scale), r, w)
        else:
            self.op("act", lambda: nc.scalar.activation(out=out, in_=in_, func=func, bias=bias, scale=scale,
                                                        accum_out=accum), r, w)

    def cp(self, eng, out, in_, r, w):
        nc = self.nc
        if eng == "act":
            self.op("act", lambda: nc.scalar.copy(out=out, in_=in_), r, w)
        else:
            e = self.engs[eng]
            self.op(eng, lambda: e.tensor_copy(out=out, in_=in_), r, w)

    def tt(self, eng, out, in0, in1, op, r, w):
        e = self.engs[eng]
        self.op(eng, lambda: e.tensor_tensor(out=out, in0=in0, in1=in1, op=op), r, w)

    def ts(self, eng, out, in0, s1, s2, op0, op1, r, w):
        e = self.engs[eng]
        if s2 is None:
            self.op(eng, lambda: e.tensor_scalar(out=out, in0=in0, scalar1=s1, scalar2=None, op0=op0), r, w)
        else:
            self.op(eng, lambda: e.tensor_scalar(out=out, in0=in0, scalar1=s1, scalar2=s2, op0=op0, op1=op1), r, w)

    def stt(self, eng, out, in0, scalar, in1, op0, op1, r, w):
        e = self.engs[eng]
        self.op(eng, lambda: e.scalar_tensor_tensor(out=out, in0=in0, scalar=scalar, in1=in1, op0=op0, op1=op1), r, w)

    def dma(self, q, out, in_, r, w, sk):
        e = self.engs[q]
        self.op(q, lambda: e.dma_start(out=out, in_=in_), r, w, dma=True, sk=sk)

    def flush(self):
        ops = self.ops
        n = len(ops)
        last_w, readers = {}, {}
        deps = [None] * n
        signal = [False] * n
        last_of_eng = {}
        for i, op in enumerate(ops):
            d_raw = set()
            d_oth = set()
            for k in op.r:
                j = last_w.get(k)
                if j is not None:
                    d_raw.add(j)
            for k in op.w:
                j = last_w.get(k)
                if j is not None:
                    d_oth.add(j)
                rd = readers.get(k)
                if rd:
                    d_oth.update(rd.values())
            d = set()
            for j in d_raw | d_oth:
                if j == i:
                    continue
                oj = ops[j]
                same = (not op.dma) and (not oj.dma) and oj.eng == op.eng
                if same:
                    if op.eng == "pe":
                        continue
                    if j not in d_raw:
                        continue
                d.add(j)
            deps[i] = sorted(d)
            for j in d:
                signal[j] = True
            for k in op.r:
                rk = ("d", i) if op.dma else op.eng
                readers.setdefault(k, {})[rk] = i
            for k in op.w:
                last_w[k] = i
                readers[k] = {}
            if not op.dma:
                last_of_eng[op.eng] = i
        for e, i in last_of_eng.items():
            signal[i] = True
        ev = [None] * n
        touched = set()
        for i, op in enumerate(ops):
            eng = self.engs[op.eng]
            kn = self.known[op.eng]
            for j in deps[i]:
                h, v, sid = ev[j]
                if kn.get(sid, 0) >= v:
                    continue
                eng.wait_ge(h, v)
                kn[sid] = v
                self.n_inst += 1
            ins = op.fn()
            self.n_inst += 1
            if op.dma or signal[i]:
                if op.dma:
                    if op.sk not in self.dmap:
                        self.dmap[op.sk] = len(self.dmap)
                    key = ("dma", self.dmap[op.sk])
                    inc = 16
                else:
                    key = ("eng", op.eng)
                    inc = 1
                ent = self.sems.get(key)
                if ent is None or ent[1] + inc > self.ROT:
                    ent = self._newsem(key)
                ent[1] += inc
                ins.then_inc(ent[0], inc)
                ev[i] = (ent[0], ent[1], ent[2])
                touched.add(ent[2])
        for ename, eng in self.engs.items():
            kn = self.known[ename]
            for sid in sorted(touched):
                h, cnt, _ = self.allsems[sid]
                if kn.get(sid, 0) < cnt:
                    eng.wait_ge(h, cnt)
                    kn[sid] = cnt
                    self.n_inst += 1
        self.ops = []
        self.dmap = {}


def host_constants(smax):
    ident = np.eye(128, dtype=np.float32)
    pos = np.arange(smax, dtype=np.float32)
    inv = (10000.0 ** (-np.arange(0, 128, 2, dtype=np.float32) / 128.0)).astype(np.float32)
    ang = pos[:, None] * inv[None, :]
    cs = np.concatenate([np.cos(ang), np.sin(ang)], axis=1).astype(np.float32)
    j = np.arange(128, dtype=np.float32)[:, None]
    i = np.arange(128, dtype=np.float32)[None, :]
    tri = np.stack([np.maximum(i - j, 0), np.maximum(j - i, 0),
                    (i >= j).astype(np.float32), (j > i).astype(np.float32)], axis=1)
    io = np.arange(128, dtype=np.float32)
    iota = np.stack([np.broadcast_to(io + 1.0, (128, 128)), np.broadcast_to(128.0 - io, (128, 128))], axis=1)
    pcol = np.stack([127.0 - io, io], axis=1)
    return dict(c_ident=ident, c_cs=cs, c_tri=np.ascontiguousarray(tri.astype(np.float32)),
                c_iota=np.ascontiguousarray(iota.astype(np.float32)),
                c_pcol=np.ascontiguousarray(pcol.astype(np.float32)))


class Builder:
    def __init__(self, s_list, depth, dbg=(), stop_after=None, alpha=None, use_moe=True, moe_stop=None):
        self.moe_stop = moe_stop
        self.alpha = float(alpha) if alpha is not None else (2.0 * depth) ** 0.25
        self.use_moe = use_moe
        self.s_list = list(s_list)
        self.depth = depth
        self.smax = max(s_list)
        self.dbg = set(dbg)
        self.stop_after = stop_after
        self.nc = bass.Bass("TRN2", target_bir_lowering=False)
        self.stack = ExitStack()
        self.S = Sched(self.nc, self.stack)
        self.uid = 0

    def din(self, name, shape, dt=F32):
        return self.nc.dram_tensor(name, list(shape), dt, kind="ExternalInput").ap()

    def dout(self, name, shape, dt=F32):
        return self.nc.dram_tensor(name, list(shape), dt, kind="ExternalOutput").ap()

    def dscr(self, name, shape, dt):
        kind = "ExternalOutput" if name in self.dbg else "Internal"
        return self.nc.dram_tensor(name, list(shape), dt, kind=kind).ap()

    def sb(self, st, name, shape, dt):
        self.uid += 1
        return st.enter_context(self.nc.sbuf_tensor(f"{name}_{self.uid}", list(shape), dt))

    def pst(self, st, name, shape, dt):
        self.uid += 1
        return st.enter_context(self.nc.psum_tensor(f"{name}_{self.uid}", list(shape), dt))

    def build(self):
        nc, S = self.nc, self.S
        L, smax = self.depth, self.smax
        self.x_in = [self.din(f"x{i}", [s, D]) for i, s in enumerate(self.s_list)]
        self.y_out = [self.dout(f"y{i}", [s, D]) for i, s in enumerate(self.s_list)]
        self.w_in = self.din("w_in", [L, D, 8192])
        self.ret_log_decay = self.din("ret_log_decay", [L, 8])
        self.ret_gn_gain = self.din("ret_gn_gain", [L, 1024])
        self.diff_lambda = self.din("diff_lambda", [L, 512])
        self.diff_subln_gain = self.din("diff_subln_gain", [L, 256])
        self.w_ret_branch = self.din("w_ret_branch", [L, 1024, 1024])
        self.w_diff_branch = self.din("w_diff_branch", [L, 1024, 1024])
        self.w_out = self.din("w_out", [L, 1024, 1024])
        self.ln1_g = self.din("ln1_g", [L, 1024])
        self.ln1_b = self.din("ln1_b", [L, 1024])
        self.router = self.din("router", [L, 1024, 36])
        if self.use_moe:
            self.w_gate_up = self.din("w_gate_up", [L, N_EXP, 1024, 1024])
            self.w_down = self.din("w_down", [L, N_EXP, 512, 1024])
        self.ln2_g = self.din("ln2_g", [L, 1024])
        self.ln2_b = self.din("ln2_b", [L, 1024])
        self.c_ident = self.din("c_ident", [128, 128])
        self.c_cs = self.din("c_cs", [smax, 128])
        self.c_tri = self.din("c_tri", [128, 4, 128])
        self.c_iota = self.din("c_iota", [128, 2, 128])
        self.c_pcol = self.din("c_pcol", [128, 2])
        self.QT = self.dscr("QT", [24, 128, smax], BF16)
        self.RKT = self.dscr("RKT", [smax, 512], BF16)
        self.RV = self.dscr("RV", [smax, 1024], BF16)
        self.SG = self.dscr("SG", [smax, 1024], BF16)
        self.DV = self.dscr("DV", [smax, 1024], BF16)
        self.GA = self.dscr("GA", [smax, 1024], BF16)
        self.GB = self.dscr("GB", [smax, 1024], BF16)
        self.BST = self.dscr("BST", [smax // 128, 128, 1024], BF16)
        self.RO = self.dscr("RO", [smax, 1024], BF16)
        self.DO = self.dscr("DO", [smax, 1024], BF16)
        self.stot = sum(self.s_list)
        self.X1 = self.dscr("X1", [self.stot, 1024], F32)
        self.XM = [self.dscr(f"XM{i}", [s, 1024], F32) for i, s in enumerate(self.s_list)]
        self.X1B = self.dscr("X1B", [self.stot, 1024], BF16)
        self.nslot_max = (2 * self.stot) // SLOT + 32
        self.XS = self.dscr("XS", [self.nslot_max * SLOT, 1024], BF16)
        self.YS = self.dscr("YS", [self.nslot_max * SLOT, 1024], F32)
        if self.use_moe:
            self.WGUB = self.dscr("WGUB", [N_EXP * 128, 8 * 1024], BF16)
            self.WDNB = self.dscr("WDNB", [N_EXP * 128, 4 * 1024], BF16)

        with self.stack:
            st = self.stack
            self.ident_f = st.enter_context(nc.sbuf_tensor("ident_f", [128, 128], F32))
            self.ident_b = st.enter_context(nc.sbuf_tensor("ident_b", [128, 128], BF16))
            S.dma("sp", self.ident_f[:], self.c_ident[:, :], [], ["ident_f"], "ident")
            S.cp("dve", self.ident_b[:], self.ident_f[:], ["ident_f"], ["ident_b"])
            self.mhalf = st.enter_context(nc.sbuf_tensor("mhalf", [128, 8], F32))
            S.op("pool", lambda: nc.gpsimd.memset(self.mhalf[:], -0.5), [], ["mhalf"])
            with ExitStack() as st0:
                zt = self.sb(st0, "zero_t", [128, 1024], BF16)
                S.op("dve", lambda: nc.vector.memset(zt[:], 0.0), [], ["zt"])
                for i in range(self.nslot_max * SLOT // 128):
                    S.dma("sp", self.XS[i * 128:(i + 1) * 128, :], zt[:], ["zt"], [("zchain", i % 4)], ("zinit", i % 4))
                S.flush()
            phases = ["p1a", "p2a", "p2b", "p3", "p4a"]
            done = False
            for l in range(L):
                for si, s in enumerate(self.s_list):
                    xsrc = self.x_in[si] if l == 0 else self.XM[si]
                    self.roff = sum(self.s_list[:si])
                    for ph in phases:
                        getattr(self, ph)(l, s, xsrc, None)
                        S.flush()
                        if self.stop_after == (l, si, ph):
                            done = True
                            break
                    if done:
                        break
                if done:
                    break
                xdsts = [self.y_out[si] if l == L - 1 else self.XM[si] for si in range(len(self.s_list))]
                self.p4b(l, xdsts)
                S.flush()
                if self.stop_after is not None and self.stop_after[0] == l and self.stop_after[2] == "p4b":
                    break
        return nc

    def p1a(self, l, s, xsrc, xdst):
        nc, S = self.nc, self.S
        nt = s // 128
        with ExitStack() as st:
            sb = lambda name, shape, dt: self.sb(st, name, shape, dt)
            wsbs = [sb(f"p1_w{i}", [128, 8, 4096], BF16) for i in range(2)]
            xs = [sb(f"p1_xs{i}", [128, 1024], F32) for i in range(2)]
            xT = [sb(f"p1_xT{i}", [128, 8, 128], BF16) for i in range(2)]
            ps = self.pst(st, "p1_ps", [128, 6, 4, 128], F32)
            cs = [sb(f"p1_cs{i}", [128, 128], F32) for i in range(2)]
            tmp = [sb(f"p1_tmp{i}", [128, 4, 4, 64], F32) for i in range(2)]
            qk = [sb(f"p1_qk{i}", [128, 24, 128], BF16) for i in range(2)]
            qTs = [sb(f"p1_qT{i}", [128, 24, 128], BF16) for i in range(2)]
            rvb = [sb(f"p1_rv{i}", [128, 1024], BF16) for i in range(2)]
            pq = self.pst(st, "p1_pq", [128, 2, 8, 128], BF16)
            sig = [sb(f"p1_sig{i}", [128, 512], F32) for i in range(2)]
            ob = {nm: [sb(f"p1_{nm}{i}", [128, 1024], BF16) for i in range(2)] for nm in ("sg", "dv", "ga", "gb")}
            wv = self.w_in[l].rearrange("(kc p) n -> p kc n", p=128)
            allgroups = [[(0, 2048, 0), (3072, 5120, 2048)], [(2048, 3072, 0), (5120, 8192, 1024)]]
            for sub_ in range(2):
                for gi, (c0, c1, o0) in enumerate(allgroups[sub_]):
                    for kc in range(8):
                        S.dma("pool", wsbs[sub_][:, kc, o0:o0 + (c1 - c0)], wv[:, kc, c0:c1], [], [("w", sub_, kc, gi)], ("w", sub_, kc, gi))
            for sub in range(2):
                self._p1_loop(l, s, xsrc, sub, nt, wsbs[sub], allgroups[sub], xs, xT, ps, cs, tmp, qk, qTs, rvb, pq, sig, ob)
            S.flush()

    def p1b(self, l, s, xsrc, xdst):
        pass

    def _p1_loop(self, l, s, xsrc, sub, nt, wsb, groups, xs, xT, ps, cs, tmp, qk, qTs, rvb, pq, sig, ob):
        nc, S = self.nc, self.S
        if True:
            if True:
                pass

            def wkeys(kc, cb):
                col = cb * 512
                gi = 0 if col < (groups[0][1] - groups[0][0]) else 1
                return ("w", sub, kc, gi)

            def load(t):
                sl = t % 2
                S.dma("sp", xs[sl][:], xsrc[t * 128:(t + 1) * 128, :], [], [("xs", sl)], ("xs", sl))
                if sub == 0:
                    S.dma("sp", cs[sl][:], self.c_cs[t * 128:(t + 1) * 128, :], [], [("cs", sl)], ("cs", sl))

            def compute(t):
                sl = t % 2
                for kc in range(8):
                    S.tr(ps[:, kc // 4, kc % 4, :], xs[sl][:, kc * 128:(kc + 1) * 128], self.ident_f[:],
                         [("xs", sl), "ident_f"], [("ps", kc // 4)])
                S.cp("act", xT[sl][:, 0:4, :], ps[:, 0], [("ps", 0)], [("xT", sl, 0)])
                S.cp("dve", xT[sl][:, 4:8, :], ps[:, 1], [("ps", 1)], [("xT", sl, 1)])
                for cb in range(8):
                    bank = 2 + cb % 4
                    pb = ps[:, bank].rearrange("p a b -> p (a b)")
                    for kc in range(8):
                        S.mm(pb, xT[sl][:, kc, :], wsb[:, kc, cb * 512:(cb + 1) * 512], kc == 0, kc == 7,
                             [("xT", sl, kc // 4), wkeys(kc, cb)], [("ps", bank)])
                    if sub == 0:
                        if cb in (2, 3):
                            S.cp("act", rvb[sl][:, (cb - 2) * 512:(cb - 1) * 512], pb, [("ps", bank)], [("rv", sl)])
                        else:
                            hb = {0: 0, 1: 4, 4: 8, 5: 12, 6: 16, 7: 20}[cb]
                            tsl = cb % 2
                            x1 = ps[:, bank, :, 0:64]
                            x2 = ps[:, bank, :, 64:128]
                            c = cs[sl][:, 0:64].unsqueeze(1).broadcast_to([128, 4, 64])
                            sn = cs[sl][:, 64:128].unsqueeze(1).broadcast_to([128, 4, 64])
                            tm = tmp[tsl]
                            S.tt("dve", tm[:, 0], x1, c, ALU.mult, [("ps", bank), ("cs", sl)], [("tmp", tsl, 0)])
                            S.tt("dve", tm[:, 1], x2, sn, ALU.mult, [("ps", bank), ("cs", sl)], [("tmp", tsl, 1)])
                            S.tt("dve", tm[:, 2], x2, c, ALU.mult, [("ps", bank), ("cs", sl)], [("tmp", tsl, 2)])
                            S.tt("dve", tm[:, 3], x1, sn, ALU.mult, [("ps", bank), ("cs", sl)], [("tmp", tsl, 3)])
                            S.tt("pool", qk[sl][:, hb:hb + 4, 0:64], tm[:, 0], tm[:, 1], ALU.subtract,
                                 [("tmp", tsl, 0), ("tmp", tsl, 1)], [("qk", sl, hb // 8)])
                            S.tt("pool", qk[sl][:, hb:hb + 4, 64:128], tm[:, 2], tm[:, 3], ALU.add,
                                 [("tmp", tsl, 2), ("tmp", tsl, 3)], [("qk", sl, hb // 8)])
                    else:
                        half = slice((cb % 2) * 512, (cb % 2 + 1) * 512)
                        if cb in (0, 1):
                            sg_t = sig[cb % 2]
                            S.act(sg_t[:], pb, AF.Sigmoid, [("ps", bank)], [("sig", cb % 2)])
                            S.tt("dve", ob["sg"][sl][:, half], sg_t[:], pb, ALU.mult,
                                 [("sig", cb % 2), ("ps", bank)], [("sg", sl)])
                        elif cb in (2, 3):
                            S.cp("act", ob["dv"][sl][:, half], pb, [("ps", bank)], [("dv", sl)])
                        elif cb in (4, 5):
                            S.act(ob["ga"][sl][:, half], pb, AF.Sigmoid, [("ps", bank)], [("ga", sl)])
                        else:
                            S.act(ob["gb"][sl][:, half], pb, AF.Sigmoid, [("ps", bank)], [("gb", sl)])
                if sub == 0:
                    for g8 in range(3):
                        pbk = g8 % 2
                        for gg in range(8):
                            g = g8 * 8 + gg
                            S.tr(pq[:, pbk, gg, :], qk[sl][:, g, :], self.ident_b[:],
                                 [("qk", sl, g // 8), "ident_b"], [("pq", pbk)])
                        S.cp("act" if g8 % 2 == 0 else "dve", qTs[sl][:, g8 * 8:(g8 + 1) * 8, :], pq[:, pbk],
                             [("pq", pbk)], [("qTs", sl, g8)])

            def store(t):
                sl = t % 2
                rows = slice(t * 128, (t + 1) * 128)
                if sub == 0:
                    S.dma("sp", self.QT[:, :, rows].rearrange("h d s -> d h s"), qTs[sl][:],
                          [("qTs", sl, 0), ("qTs", sl, 1), ("qTs", sl, 2)], [], ("st_q", sl))
                    S.dma("sp", self.RKT[rows, :].rearrange("p (h d) -> p h d", h=4), qk[sl][:, 4:8, :],
                          [("qk", sl, 0)], [], ("st_k", sl))
                    S.dma("sp", self.RV[rows, :], rvb[sl][:], [("rv", sl)], [], ("st_v", sl))
                else:
                    for nm, dst in (("sg", self.SG), ("dv", self.DV), ("ga", self.GA), ("gb", self.GB)):
                        S.dma("sp", dst[rows, :], ob[nm][sl][:], [(nm, sl)], [], ("st_" + nm, sl))

            load(0)
            for t in range(nt):
                if t + 1 < nt:
                    load(t + 1)
                compute(t)
                store(t)
            S.flush()

    def _p2_tables(self, l, st, full):
        nc, S = self.nc, self.S
        sb = lambda name, shape, dt: self.sb(st, name, shape, dt)
        T = {}
        ld = sb("p2_ld", [128, 8], F32)
        nl = sb("p2_nl", [128, 8], F32)
        pcol = sb("p2_pcol", [128, 2], F32)
        S.dma("sp", ld[:], self.ret_log_decay[l:l + 1, :].broadcast_to([128, 8]), [], ["ld"], "ld")
        S.dma("sp", pcol[:], self.c_pcol[:, :], [], ["pcol"], "pcol")
        S.ts("dve", nl[:], ld[:], -1.0, None, ALU.mult, None, ["ld"], ["nl0"])
        S.tt("dve", nl[:], nl[:], ld[:], ALU.min, ["nl0", "ld"], ["nl"])
        Z = sb("p2_Z", [128, 8], F32)
        DEC = sb("p2_DEC", [128, 8], F32)
        for h in range(4):
            S.act(Z[:, h:h + 1], pcol[:, 0:1], AF.Exp, ["pcol", "nl"], [("Zr", h)], scale=nl[:, h:h + 1])
            S.act(Z[:, 4 + h:5 + h], pcol[:, 1:2], AF.Exp, ["pcol", "nl"], [("Zr", 4 + h)], scale=nl[:, 4 + h:5 + h])
        S.ts("dve", Z[:], Z[:], 128.0 ** -0.5, None, ALU.mult, None, [("Zr", i) for i in range(8)], ["Z"])
        S.act(DEC[:], nl[:], AF.Exp, ["nl"], ["DEC"], scale=128.0)
        T.update(Z=Z, DEC=DEC)
        if full:
            tri = sb("p2_tri", [128, 4, 128], F32)
            iota = sb("p2_iota", [128, 2, 128], F32)
            S.dma("sp", tri[:], self.c_tri[:, :, :], [], ["tri"], "tri")
            S.dma("sp", iota[:], self.c_iota[:, :, :], [], ["iota"], "iota")
            MT = sb("p2_MT", [128, 4, 128], F32)
            XIF = sb("p2_XIF", [128, 4, 128], BF16)
            XIB = sb("p2_XIB", [128, 4, 128], BF16)
            ta = sb("p2_ta", [128, 128], F32)
            tb = sb("p2_tb", [128, 128], F32)
            for h in range(4):
                S.act(ta[:], tri[:, 0, :], AF.Exp, ["tri", "nl"], ["ta"], scale=nl[:, h:h + 1])
                S.tt("dve", ta[:], ta[:], tri[:, 2, :], ALU.mult, ["ta", "tri"], ["ta"])
                S.act(tb[:], tri[:, 1, :], AF.Exp, ["tri", "nl"], ["tb"], scale=nl[:, 4 + h:5 + h])
                S.tt("dve", tb[:], tb[:], tri[:, 3, :], ALU.mult, ["tb", "tri"], ["tb"])
                S.tt("dve", ta[:], ta[:], tb[:], ALU.add, ["ta", "tb"], ["ta"])
                S.ts("dve", MT[:, h, :], ta[:], 128.0 ** -0.5, None, ALU.mult, None, ["ta"], [("MT", h)])
                S.act(XIF[:, h, :], iota[:, 0, :], AF.Exp, ["iota", "nl"], [("XIF", h)], scale=nl[:, h:h + 1])
                S.act(XIB[:, h, :], iota[:, 1, :], AF.Exp, ["iota", "nl"], [("XIB", h)], scale=nl[:, 4 + h:5 + h])
            GN = sb("p2_GN", [128, 1024], F32)
            S.dma("sp", GN[:], self.ret_gn_gain[l:l + 1, :].broadcast_to([128, 1024]), [], ["GN"], "GN")
            T.update(MT=MT, XIF=XIF, XIB=XIB, GN=GN)
        return T

    def _state_update(self, st_f, st_bf_next, kt, v, kz, pst, zcol, dcol, T, sl, keys):
        S = self.S
        Z, DEC = T["Z"], T["DEC"]
        S.tt("dve", kz[:], kt.rearrange("p (h d) -> p h d", h=4),
             Z[:, zcol:zcol + 4].unsqueeze(2).broadcast_to([128, 4, 128]), ALU.mult,
             [keys["kt"], "Z"], [("kz", sl)])
        for h in range(4):
            S.mm(pst[:, h, :], kz[:, h, :], v[:, h * 256:(h + 1) * 256], True, True,
                 [("kz", sl), keys["v"]], [("pst", h // 2)])
        S.tt("pool", st_f[:], st_f[:], DEC[:, dcol:dcol + 4].unsqueeze(2).broadcast_to([128, 4, 256]), ALU.mult,
             ["stf", "DEC"], ["stf"])
        S.tt("dve", st_f[:], st_f[:], pst[:], ALU.add, ["stf", ("pst", 0), ("pst", 1)], ["stf"])
        S.cp("act", st_bf_next[0][:], st_f[:], ["stf"], [st_bf_next[1]])

    def p2a(self, l, s, xsrc, xdst):
        nc, S = self.nc, self.S
        nt = s // 128
        with ExitStack() as st:
            sb = lambda name, shape, dt: self.sb(st, name, shape, dt)
            T = self._p2_tables(l, st, False)
            kt = [sb(f"p2a_kt{i}", [128, 512], BF16) for i in range(2)]
            v = [sb(f"p2a_v{i}", [128, 1024], BF16) for i in range(2)]
            kz = [sb(f"p2a_kz{i}", [128, 4, 128], BF16) for i in range(2)]
            stf = sb("p2a_stf", [128, 4, 256], F32)
            stbf = [sb(f"p2a_stbf{i}", [128, 4, 256], BF16) for i in range(2)]
            pst = self.pst(st, "p2a_pst", [128, 4, 256], F32)
            S.op("dve", lambda: nc.vector.memset(stf[:], 0.0), [], ["stf"])
            S.op("dve", lambda: nc.vector.memset(stbf[0][:], 0.0), [], [("stbf", 0)])

            def load(c):
                sl = c % 2
                rows = slice(c * 128, (c + 1) * 128)
                S.dma("sp", kt[sl][:], self.RKT[rows, :], [], [("kt", sl)], ("kt", sl))
                S.dma("sp", v[sl][:], self.RV[rows, :], [], [("v", sl)], ("v", sl))

            order = list(range(nt - 1, -1, -1))
            if nt > 1:
                load(order[0])
            for idx, c in enumerate(order):
                cur = idx % 2
                S.dma("sp", self.BST[c], stbf[cur][:].rearrange("p h e -> p (h e)"), [("stbf", cur)], [], ("st_b", cur))
                if idx + 1 < nt:
                    if idx + 1 < nt - 1:
                        load(order[idx + 1])
                    sl = c % 2
                    self._state_update(stf, (stbf[1 - cur], ("stbf", 1 - cur)), kt[sl][:], v[sl], kz[sl], pst, 4, 4, T, sl,
                                       dict(kt=("kt", sl), v=("v", sl)))
            S.flush()

    def p2b(self, l, s, xsrc, xdst):
        nc, S = self.nc, self.S
        nt = s // 128
        with ExitStack() as st:
            sb = lambda name, shape, dt: self.sb(st, name, shape, dt)
            T = self._p2_tables(l, st, True)
            MT, XIF, XIB, GN = T["MT"], T["XIF"], T["XIB"], T["GN"]
            qT = [sb(f"p2_qT{i}", [128, 4, 128], BF16) for i in range(2)]
            kT = [sb(f"p2_kT{i}", [128, 4, 128], BF16) for i in range(2)]
            kt = [sb(f"p2_kt{i}", [128, 512], BF16) for i in range(2)]
            v = [sb(f"p2_v{i}", [128, 1024], BF16) for i in range(2)]
            sg = [sb(f"p2_sg{i}", [128, 1024], BF16) for i in range(2)]
            Bs = [sb(f"p2_B{i}", [128, 4, 256], BF16) for i in range(2)]
            qxf = [sb(f"p2_qxf{i}", [128, 4, 128], BF16) for i in range(2)]
            qxb = [sb(f"p2_qxb{i}", [128, 4, 128], BF16) for i in range(2)]
            pT = [sb(f"p2_pT{i}", [128, 4, 128], BF16) for i in range(2)]
            kz = [sb(f"p2_kz{i}", [128, 4, 128], BF16) for i in range(2)]
            on = [sb(f"p2_on{i}", [128, 1024], F32) for i in range(2)]
            ro = [sb(f"p2_ro{i}", [128, 1024], BF16) for i in range(2)]
            stats = sb("p2_stats", [128, 4, 6], F32)
            mv = sb("p2_mv", [128, 4, 2], F32)
            rstd = sb("p2_rstd", [128, 4], F32)
            nb = sb("p2_nb", [128, 4], F32)
            stf = sb("p2_stf", [128, 4, 256], F32)
            stbf = [sb(f"p2_stbf{i}", [128, 4, 256], BF16) for i in range(2)]
            pss = [self.pst(st, f"p2_pss{i}", [128, 4, 128], F32) for i in range(2)]
            po = [self.pst(st, f"p2_po{i}", [128, 4, 256], F32) for i in range(2)]
            pst = self.pst(st, "p2_pst", [128, 4, 256], F32)
            S.op("dve", lambda: nc.vector.memset(stf[:], 0.0), [], ["stf"])
            S.op("dve", lambda: nc.vector.memset(stbf[0][:], 0.0), [], [("stbf", 0)])

            def load(c):
                sl = c % 2
                rows = slice(c * 128, (c + 1) * 128)
                S.dma("sp", qT[sl][:], self.QT[0:4, :, rows].rearrange("h d s -> d h s"), [], [("qT", sl)], ("qT", sl))
                S.dma("sp", kT[sl][:], self.QT[4:8, :, rows].rearrange("h d s -> d h s"), [], [("kT", sl)], ("kT", sl))
                S.dma("sp", kt[sl][:], self.RKT[rows, :], [], [("kt", sl)], ("kt", sl))
                S.dma("sp", v[sl][:], self.RV[rows, :], [], [("v", sl)], ("v", sl))
                S.dma("sp", sg[sl][:], self.SG[rows, :], [], [("sg", sl)], ("sg", sl))
                S.dma("sp", Bs[sl][:].rearrange("p h e -> p (h e)"), self.BST[c], [], [("Bs", sl)], ("Bs", sl))

            def compute(c):
                sl = c % 2
                cur = c % 2
                S.tt("dve", qxf[sl][:], qT[sl][:], XIF[:], ALU.mult, [("qT", sl)] + [("XIF", h) for h in range(4)], [("qxf", sl)])
                S.tt("pool", qxb[sl][:], qT[sl][:], XIB[:], ALU.mult, [("qT", sl)] + [("XIB", h) for h in range(4)], [("qxb", sl)])
                for h in range(4):
                    S.mm(pss[sl][:, h, :], kT[sl][:, h, :], qT[sl][:, h, :], True, True, [("kT", sl), ("qT", sl)], [("pss", sl)])
                S.tt("dve", pT[sl][:], pss[sl][:], MT[:], ALU.mult, [("pss", sl)] + [("MT", h) for h in range(4)], [("pT", sl)])
                for h in range(4):
                    pk = ("po", sl, h // 2)
                    vv = v[sl][:, h * 256:(h + 1) * 256]
                    S.mm(po[sl][:, h, :], pT[sl][:, h, :], vv, True, False, [("pT", sl), ("v", sl)], [pk])
                    S.mm(po[sl][:, h, :], qxf[sl][:, h, :], stbf[cur][:, h, :], False, False, [("qxf", sl), ("stbf", cur)], [pk])
                    S.mm(po[sl][:, h, :], qxb[sl][:, h, :], Bs[sl][:, h, :], False, True, [("qxb", sl), ("Bs", sl)], [pk])
                for h in range(4):
                    S.op("dve", (lambda h=h: nc.vector.bn_stats(out=stats[:, h, :], in_=po[sl][:, h, :])),
                         [("po", sl, h // 2)], [("stats", h)])
                for h in range(4):
                    S.op("dve", (lambda h=h: nc.vector.bn_aggr(out=mv[:, h, :], in_=stats[:, h, :])), [("stats", h)], [("mv", h)])
                mvk = [("mv", h) for h in range(4)]
                S.ts("dve", rstd[:], mv[:, :, 1], EPS, None, ALU.add, None, mvk, ["rstd0"])
                S.tt("pool", rstd[:], rstd[:], self.mhalf[:, 0:4], ALU.pow, ["rstd0", "mhalf"], ["rstd"])
                S.stt("dve", nb[:], mv[:, :, 0], -1.0, rstd[:], ALU.mult, ALU.mult, mvk + ["rstd"], ["nb"])
                for h in range(4):
                    S.act(on[sl][:, h * 256:(h + 1) * 256], po[sl][:, h, :], AF.Identity, [("po", sl, h // 2), "rstd", "nb"],
                          [("on", sl, h)], scale=rstd[:, h:h + 1], bias=nb[:, h:h + 1])
                onk = [("on", sl, h) for h in range(4)]
                S.tt("pool", on[sl][:], on[sl][:], GN[:], ALU.mult, onk + ["GN"], [("on2", sl)])
                S.tt("dve", ro[sl][:], on[sl][:], sg[sl][:], ALU.mult, [("on2", sl), ("sg", sl)] + onk, [("ro", sl)])
                S.dma("sp", self.RO[c * 128:(c + 1) * 128, :], ro[sl][:], [("ro", sl)], [], ("st_ro", sl))
                if c + 1 < nt:
                    self._state_update(stf, (stbf[1 - cur], ("stbf", 1 - cur)), kt[sl][:], v[sl], kz[sl], pst, 0, 0, T, sl,
                                       dict(kt=("kt", sl), v=("v", sl)))

            load(0)
            for c in range(nt):
                if c + 1 < nt:
                    load(c + 1)
                compute(c)
            S.flush()

    def p3(self, l, s, xsrc, xdst):
        nc, S = self.nc, self.S
        nt = s // 128
        QB = 256
        lam_init = 0.8 - 0.6 * math.exp(-0.3 * l)
        with ExitStack() as st:
            sb = lambda name, shape, dt: self.sb(st, name, shape, dt)
            dl = sb("p3_dl", [128, 2, 2, 128], F32)
            prod = sb("p3_prod", [128, 2, 128], F32)
            sm = sb("p3_sm", [128, 2], F32)
            ee = sb("p3_ee", [128, 2], F32)
            nlam = sb("p3_nlam", [128, 1], F32)
            SUB = sb("p3_SUB", [128, 256], F32)
            S.dma("sp", dl[:].rearrange("p a b d -> p (a b d)"), self.diff_lambda[l:l + 1, :].broadcast_to([128, 512]), [], ["dl"], "dl")
            S.dma("sp", SUB[:], self.diff_subln_gain[l:l + 1, :].broadcast_to([128, 256]), [], ["SUBr"], "SUB")
            S.tt("dve", prod[:], dl[:, :, 0, :], dl[:, :, 1, :], ALU.mult, ["dl"], ["prod"])
            S.op("dve", lambda: nc.vector.reduce_sum(out=sm[:], in_=prod[:], axis=AX.X), ["prod"], ["sm"])
            S.act(ee[:], sm[:], AF.Exp, ["sm"], ["ee"])
            S.tt("dve", nlam[:], ee[:, 1:2], ee[:, 0:1], ALU.subtract, ["ee"], ["nlam0"])
            S.ts("dve", nlam[:], nlam[:], -lam_init, None, ALU.add, None, ["nlam0"], ["nlam"])
            S.ts("dve", SUB[:], SUB[:], 1.0 - lam_init, None, ALU.mult, None, ["SUBr"], ["SUB"])

            kTs = sb("p3_kT", [128, 2, s], BF16)
            Vs = sb("p3_V", [128, nt, 257], BF16)
            qT = [sb(f"p3_qT{i}", [128, 2, QB], BF16) for i in range(2)]
            pT = [sb(f"p3_pT{i}", [128, 2, 512], BF16) for i in range(2)]
            dsb = [sb(f"p3_do{i}", [128, 2, 256], BF16) for i in range(2)]
            accs = sb("p3_accs", [128, 4, 257], F32)
            rs = sb("p3_rs", [128, 2], F32)
            rs2 = sb("p3_rs2", [128, 1], F32)
            d1 = sb("p3_d1", [128, 256], F32)
            dd = sb("p3_dd", [128, 256], F32)
            junk = sb("p3_junk", [128, 256], F32)
            ss = sb("p3_ss", [128, 1], F32)
            rq = sb("p3_rq", [128, 1], F32)
            pss = [self.pst(st, f"p3_pss{i}", [128, 2, 512], F32) for i in range(2)]
            acc = self.pst(st, "p3_acc", [128, 4, 512], F32)
            S.op("pool", lambda: nc.gpsimd.memset(Vs[:, :, 256:257], 1.0), [], ["Vones"])
            if self.use_moe and s == self.s_list[0]:
                for e in range(N_EXP):
                    gsrc = self.w_gate_up[l, e].rearrange("(kc p) n -> p kc n", p=128)
                    gdst = self.WGUB[e * 128:(e + 1) * 128, :].rearrange("p (kc n) -> p kc n", kc=8)
                    for hf in range(2):
                        S.dma("pool", gdst[:, 4 * hf:4 * hf + 4, :], gsrc[:, 4 * hf:4 * hf + 4, :], [], [("pchain", hf)], ("precast", hf))
                    dsrc = self.w_down[l, e].rearrange("(j p) n -> p j n", p=128)
                    ddst = self.WDNB[e * 128:(e + 1) * 128, :].rearrange("p (j n) -> p j n", j=4)
                    S.dma("pool", ddst, dsrc, [], [("pchain", 2)], ("precast", 2))
            npair = nt // 2
            scale = 1.0 / math.sqrt(128.0)
            nqb = s // QB
            qcount = 0
            for h in range(4):
                for m in range(2):
                    S.dma("sp", kTs[:, m, :], self.QT[16 + 2 * h + m, :, 0:s], [], [("kT", m)], ("kT", m))
                S.dma("sp", Vs[:, :, 0:256], self.DV[0:s, h * 256:(h + 1) * 256].rearrange("(t p) c -> p t c", p=128),
                      [], ["V"], "V")

                def loadq(qb, sl, h=h):
                    S.dma("sp", qT[sl][:], self.QT[8 + 2 * h:8 + 2 * h + 2, :, qb * QB:(qb + 1) * QB].rearrange("m d s -> d m s"),
                          [], [("qT", sl)], ("qT", sl))

                loadq(0, qcount % 2)
                for qb in range(nqb):
                    sl = qcount % 2
                    if qb + 1 < nqb:
                        loadq(qb + 1, (qcount + 1) % 2)

                    def scores(j, sl=sl):
                        p = j % 2
                        for kk in range(2):
                            ktile = 2 * j + kk
                            for m in range(2):
                                S.mm(pss[p][:, kk, m * 256:(m + 1) * 256], kTs[:, m, ktile * 128:(ktile + 1) * 128], qT[sl][:, m, :],
                                     True, True, [("kT", m), ("qT", sl)], [("pss", p)])

                    def expo(j):
                        p = j % 2
                        S.act(pT[p][:], pss[p][:], AF.Exp, [("pss", p)], [("pT", p)], scale=scale)

                    def av(j):
                        p = j % 2
                        for kk in range(2):
                            ktile = 2 * j + kk
                            for m in range(2):
                                for qt in range(2):
                                    a = m * 2 + qt
                                    S.mm(acc[:, a, 0:257], pT[p][:, kk, m * 256 + qt * 128:m * 256 + (qt + 1) * 128], Vs[:, ktile, :],
                                         ktile == 0, ktile == nt - 1, [("pT", p), "V", "Vones"], [("acc", a)])

                    scores(0)
                    for j in range(npair):
                        if j + 1 < npair:
                            scores(j + 1)
                        expo(j)
                        av(j)
                    ds = dsb[sl]
                    S.cp("act", accs[:, 0:2, :], acc[:, 0:2, 0:257], [("acc", 0), ("acc", 1)], [("accs", 0), ("accs", 1)])
                    S.cp("dve", accs[:, 2:4, :], acc[:, 2:4, 0:257], [("acc", 2), ("acc", 3)], [("accs", 2), ("accs", 3)])
                    for qt in range(2):
                        a1 = accs[:, qt, :]
                        a2 = accs[:, 2 + qt, :]
                        S.op("dve", (lambda a1=a1: nc.vector.reciprocal(out=rs[:, 0:1], in_=a1[:, 256:257])), [("accs", qt)], [("rs", 0)])
                        S.op("dve", (lambda a2=a2: nc.vector.reciprocal(out=rs[:, 1:2], in_=a2[:, 256:257])), [("accs", 2 + qt)], [("rs", 1)])
                        S.tt("dve", rs2[:], rs[:, 1:2], nlam[:], ALU.mult, [("rs", 1), "nlam"], ["rs2"])
                        S.act(d1[:], a1[:, 0:256], AF.Identity, [("accs", qt), ("rs", 0)], ["d1"], scale=rs[:, 0:1])
                        S.stt("dve", dd[:], a2[:, 0:256], rs2[:, 0:1], d1[:], ALU.mult, ALU.add, [("accs", 2 + qt), "rs2", "d1"], ["dd"])
                        S.act(junk[:], dd[:], AF.Square, ["dd"], ["junk"], accum=ss[:])
                        S.ts("dve", rq[:], ss[:], 1.0 / 256.0, EPS, ALU.mult, ALU.add, ["junk"], ["rq0"])
                        S.tt("pool", rq[:], rq[:], self.mhalf[:, 0:1], ALU.pow, ["rq0", "mhalf"], ["rq"])
                        S.stt("dve", ds[:, qt, :], dd[:], rq[:, 0:1], SUB[:], ALU.mult, ALU.mult, ["dd", "rq", "SUB"], [("ds", sl, qt)])
                    S.dma("sp", self.DO[qb * QB:(qb + 1) * QB, h * 256:(h + 1) * 256].rearrange("(q p) c -> p q c", p=128), ds[:],
                          [("ds", sl, 0), ("ds", sl, 1)], [], ("st_do", sl))
                    qcount += 1
            S.flush()

    def _ln(self, eng_r, r, stats, mv, rstd, nb, xn, G, Bv, out, rk, outk, tail="pool"):
        nc, S = self.nc, self.S
        S.op("dve", lambda: nc.vector.bn_stats(out=stats[:, 0, :], in_=r[:, 0:512]), rk, [("lnst", 0)])
        S.op("dve", lambda: nc.vector.bn_stats(out=stats[:, 1, :], in_=r[:, 512:1024]), rk, [("lnst", 1)])
        S.op("dve", lambda: nc.vector.bn_aggr(out=mv[:], in_=stats[:].rearrange("p a b -> p (a b)")),
             [("lnst", 0), ("lnst", 1)], ["lnmv"])
        S.ts("dve", rstd[:], mv[:, 1:2], EPS, None, ALU.add, None, ["lnmv"], ["lnrstd0"])
        S.tt("pool", rstd[:], rstd[:], self.mhalf[:, 0:1], ALU.pow, ["lnrstd0", "mhalf"], ["lnrstd"])
        S.stt("dve", nb[:], mv[:, 0:1], -1.0, rstd[:], ALU.mult, ALU.mult, ["lnmv", "lnrstd"], ["lnnb"])
        S.act(xn[:], r[:], AF.Identity, rk + ["lnrstd", "lnnb"], ["lnxn"], scale=rstd[:, 0:1], bias=nb[:, 0:1])
        S.tt(tail, xn[:], xn[:], G[:], ALU.mult, ["lnxn", "LNG"], ["lnxn2"])
        S.tt(tail, out, xn[:], Bv[:], ALU.add, ["lnxn2", "lnxn", "LNB"], outk)

    def p4a(self, l, s, xsrc, xdst):
        nc, S = self.nc, self.S
        nt = s // 128
        alpha = self.alpha
        with ExitStack() as st:
            sb = lambda name, shape, dt: self.sb(st, name, shape, dt)
            W = {}
            for nm, src in (("rb", self.w_ret_branch), ("db", self.w_diff_branch), ("wo", self.w_out)):
                W[nm] = sb("p4_w" + nm, [128, 8, 1024], BF16)
                wv = src[l].rearrange("(kc p) n -> p kc n", p=128)
                for kc in range(8):
                    S.dma("pool", W[nm][:, kc, :], wv[:, kc, :], [], [(nm, kc)], (nm, kc))
            G = sb("p4_G", [128, 1024], F32)
            Bv = sb("p4_B", [128, 1024], F32)
            S.dma("sp", G[:], self.ln1_g[l:l + 1, :].broadcast_to([128, 1024]), [], ["LNG"], "LNG")
            S.dma("sp", Bv[:], self.ln1_b[l:l + 1, :].broadcast_to([128, 1024]), [], ["LNB"], "LNB")
            inb = {nm: [sb(f"p4_{nm}{i}", [128, 1024], BF16) for i in range(2)] for nm in ("ro", "do", "ga", "gb")}
            xs = [sb(f"p4_xs{i}", [128, 1024], F32) for i in range(2)]
            roT = [sb(f"p4_roT{i}", [128, 8, 128], BF16) for i in range(2)]
            doT = [sb(f"p4_doT{i}", [128, 8, 128], BF16) for i in range(2)]
            mgT = [sb(f"p4_mgT{i}", [128, 8, 128], BF16) for i in range(2)]
            m1 = sb("p4_m1", [128, 1024], F32)
            m2 = sb("p4_m2", [128, 1024], F32)
            mg = sb("p4_mg", [128, 1024], BF16)
            r = sb("p4_r", [128, 1024], F32)
            xn = sb("p4_xn", [128, 1024], F32)
            x1s = [sb(f"p4_x1s{i}", [128, 1024], F32) for i in range(2)]
            x1b = [sb(f"p4_x1b{i}", [128, 1024], BF16) for i in range(2)]
            stats = sb("p4_stats", [128, 2, 6], F32)
            mv = sb("p4_mv", [128, 2], F32)
            rstd = sb("p4_rstd", [128, 1], F32)
            nb = sb("p4_nb", [128, 1], F32)
            pq = self.pst(st, "p4_pq", [128, 2, 8, 128], BF16)
            pA = self.pst(st, "p4_pA", [128, 2, 512], F32)
            pB = self.pst(st, "p4_pB", [128, 2, 512], F32)
            pM = self.pst(st, "p4_pM", [128, 2, 512], F32)
            srcs = dict(ro=self.RO, do=self.DO, ga=self.GA, gb=self.GB)

            def load(t):
                sl = t % 2
                rows = slice(t * 128, (t + 1) * 128)
                for nm in ("ro", "do", "ga", "gb"):
                    S.dma("sp", inb[nm][sl][:], srcs[nm][rows, :], [], [(nm, sl)], (nm, sl))
                S.dma("sp", xs[sl][:], xsrc[rows, :], [], [("xs", sl)], ("xs", sl))

            def compute(t):
                sl = t % 2
                for kc in range(8):
                    S.tr(pq[:, 0, kc, :], inb["ro"][sl][:, kc * 128:(kc + 1) * 128], self.ident_b[:], [("ro", sl), "ident_b"], [("pq", 0)])
                S.cp("act", roT[sl][:], pq[:, 0], [("pq", 0)], [("roT", sl)])
                for kc in range(8):
                    S.tr(pq[:, 1, kc, :], inb["do"][sl][:, kc * 128:(kc + 1) * 128], self.ident_b[:], [("do", sl), "ident_b"], [("pq", 1)])
                S.cp("dve", doT[sl][:], pq[:, 1], [("pq", 1)], [("doT", sl)])
                for cb in range(2):
                    for kc in range(8):
                        S.mm(pA[:, cb, :], roT[sl][:, kc, :], W["rb"][:, kc, cb * 512:(cb + 1) * 512], kc == 0, kc == 7,
                             [("roT", sl), ("rb", kc)], [("pA", cb)])
                for cb in range(2):
                    for kc in range(8):
                        S.mm(pB[:, cb, :], doT[sl][:, kc, :], W["db"][:, kc, cb * 512:(cb + 1) * 512], kc == 0, kc == 7,
                             [("doT", sl), ("db", kc)], [("pB", cb)])
                S.tt("dve", m1[:], inb["ga"][sl][:], pA[:].rearrange("p a b -> p (a b)"), ALU.mult,
                     [("ga", sl), ("pA", 0), ("pA", 1)], ["m1"])
                S.tt("dve", m2[:], inb["gb"][sl][:], pB[:].rearrange("p a b -> p (a b)"), ALU.mult,
                     [("gb", sl), ("pB", 0), ("pB", 1)], ["m2"])
                S.tt("dve", mg[:], m1[:], m2[:], ALU.add, ["m1", "m2"], ["mg"])
                for kc in range(8):
                    S.tr(pq[:, 0, kc, :], mg[:, kc * 128:(kc + 1) * 128], self.ident_b[:], ["mg", "ident_b"], [("pq", 0)])
                S.cp("act", mgT[sl][:], pq[:, 0], [("pq", 0)], [("mgT", sl)])
                for cb in range(2):
                    for kc in range(8):
                        S.mm(pM[:, cb, :], mgT[sl][:, kc, :], W["wo"][:, kc, cb * 512:(cb + 1) * 512], kc == 0, kc == 7,
                             [("mgT", sl), ("wo", kc)], [("pM", cb)])
                S.stt("dve", r[:], xs[sl][:], alpha, pM[:].rearrange("p a b -> p (a b)"), ALU.mult, ALU.add,
                      [("xs", sl), ("pM", 0), ("pM", 1)], ["r"])
                self._ln("dve", r, stats, mv, rstd, nb, xn, G, Bv, x1s[sl][:], ["r"], [("x1s", sl)], tail="dve")
                S.dma("sp", self.X1[self.roff + t * 128:self.roff + (t + 1) * 128, :], x1s[sl][:], [("x1s", sl)], [], ("st_x1", sl))
                S.cp("act", x1b[sl][:], x1s[sl][:], [("x1s", sl)], [("x1b", sl)])
                S.dma("sp", self.X1B[self.roff + t * 128:self.roff + (t + 1) * 128, :], x1b[sl][:], [("x1b", sl)], [], ("st_x1b", sl))

            load(0)
            for t in range(nt):
                if t + 1 < nt:
                    load(t + 1)
                compute(t)
            S.flush()

    def p4b(self, l, xdsts):
        nc, S = self.nc, self.S
        V = nc.vector
        alpha = self.alpha
        s = self.stot
        nt = s // 128
        nslot = (2 * s) // SLOT + 32
        J = max(s // SLOT, 1)
        tile_dst = []
        for si_q, sq in enumerate(self.s_list):
            for tq in range(sq // 128):
                tile_dst.append((xdsts[si_q], tq))
        with ExitStack() as st:
            sb = lambda name, shape, dt: self.sb(st, name, shape, dt)
            wr = sb("r_wr", [128, 8, 36], BF16)
            S.dma("pool", wr[:], self.router[l].rearrange("(kc p) n -> p kc n", p=128), [], ["wr"], "wr")
            G = sb("r_G", [128, 1024], F32)
            Bv = sb("r_B", [128, 1024], F32)
            S.dma("sp", G[:], self.ln2_g[l:l + 1, :].broadcast_to([128, 1024]), [], ["LNG"], "LNG")
            S.dma("sp", Bv[:], self.ln2_b[l:l + 1, :].broadcast_to([128, 1024]), [], ["LNB"], "LNB")
            tri = sb("r_tri", [128, 128], F32)
            iota = sb("r_iota", [128, 128], F32)
            S.dma("sp", tri[:], self.c_tri[:, 0, :], [], ["tri"], "tri")
            S.dma("sp", iota[:], self.c_iota[:, 0, :], [], ["iota"], "iota")
            U = sb("r_U", [128, 128], BF16)
            onesb = sb("r_ones", [128, 128], BF16)
            S.ts("dve", U[:], tri[:], 0.0, None, ALU.is_gt, None, ["tri"], ["U"])
            S.op("dve", lambda: V.memset(onesb[:], 1.0), [], ["onesb"])
            MK = sb("r_MK", [128, nt, 2, 32], F32)
            RK = sb("r_RK", [128, nt, 2], F32)
            WG = sb("r_WG", [128, nt, 2], F32)
            PI = sb("r_PI", [128, nt, 2], I32)
            Acum = sb("r_Acum", [128, 32], F32)
            AcumB = sb("r_AcumB", [128, 32], BF16)
            Ab = sb("r_Ab", [128, 32], BF16)
            eidi = sb("r_eidi", [128, 128], I32)
            pcol = sb("r_pcol", [128, 2], F32)
            S.dma("sp", pcol[:], self.c_pcol[:, :], [], ["pcol"], "pcol")
            base = sb("r_base", [128, 32], F32)
            S.op("dve", lambda: V.memset(Acum[:], 0.0), [], ["Acum"])
            S.op("dve", lambda: V.memset(AcumB[:], 0.0), [], ["AcumB"])
            xs = [sb(f"r_xs{i}", [128, 1024], F32) for i in range(2)]
            x1T = [sb(f"r_x1T{i}", [128, 8, 128], BF16) for i in range(2)]
            xb = [sb(f"r_xb{i}", [128, 1024], BF16) for i in range(2)]
            rl = sb("r_rl", [128, 36], F32)
            sm = {nm: sb("r_" + nm, [128, 1], F32) for nm in
                  ("gmax", "ngmax", "gsum", "gp", "m1", "m2", "dd", "ed", "den", "w1", "w2")}
            oh = sb("r_oh", [128, 4], F32)
            eg = sb("r_eg", [128, 4], F32)
            t3 = sb("r_t3", [128, 4, 8], F32)
            esel = sb("r_esel", [128, 8], F32)
            mk1 = sb("r_mk1", [128, 8], F32)
            mk2 = sb("r_mk2", [128, 8], F32)
            e2 = sb("r_e2", [128, 8], F32)
            t2 = sb("r_t2", [128, 2, 32], F32)
            posf = sb("r_posf", [128, 2], F32)
            big = sb("r_big", [128, 128 * 32], F32)
            ntot = sb("r_ntot", [128, 32], F32)
            nte = sb("r_nte", [128, 32], F32)
            thr = sb("r_thr", [128, 128], F32)
            cinc = sb("r_cinc", [128, 32], F32)
            eidf = sb("r_eidf", [128, 128], F32)
            gu = [sb(f"r_gu{i}", [128, 8, 1024], BF16) for i in range(2)]
            dn = [sb(f"r_dn{i}", [128, 4, 1024], BF16) for i in range(2)]
            xsl = [[sb(f"r_xsl{i}{q}", [128, 1024], BF16) for q in range(2)] for i in range(2)]
            xT = [sb(f"r_xT{i}", [128, 8, SLOT], BF16) for i in range(2)]
            actT = [sb(f"r_actT{i}", [128, 4, SLOT], BF16) for i in range(2)]
            slu = [sb(f"r_slu{i}", [128, SLOT], F32) for i in range(2)]
            yo = [sb(f"r_yo{i}", [128, 1024], F32) for i in range(2)]
            g1 = [sb(f"r_g1{i}", [128, 1024], F32) for i in range(2)]
            g2 = [sb(f"r_g2{i}", [128, 1024], F32) for i in range(2)]
            r = sb("r_r", [128, 1024], F32)
            xn = sb("r_xn", [128, 1024], F32)
            xo = [sb(f"r_xo{i}", [128, 1024], F32) for i in range(2)]
            stats = sb("r_stats", [128, 2, 6], F32)
            mv = sb("r_mv", [128, 2], F32)
            rstd = sb("r_rstd", [128, 1], F32)
            nb = sb("r_nb", [128, 1], F32)
            pq = self.pst(st, "r_pq", [128, 2, 8, 128], BF16)
            pgu = self.pst(st, "r_pgu", [128, 2, 512], F32)
            pdn = self.pst(st, "r_pdn", [128, 4, 512], F32)

            G_ = 8 if nt % 8 == 0 else (4 if nt % 4 == 0 else (2 if nt % 2 == 0 else 1))
            rlG = sb("r_rlG", [128, G_, 36], F32)
            f2 = {nm: sb("r_g_" + nm, [128, G_], F32) for nm in ("gmax", "gsum", "gp", "m1", "m2", "dd", "ed", "den", "w1", "w2")}
            ohG = sb("r_ohG", [128, G_, 4], F32)
            lgsG = sb("r_lgsG", [128, G_, 4], F32)
            egG = sb("r_egG", [128, G_, 4], F32)
            t3G = sb("r_t3G", [128, G_, 4, 8], F32)
            eselG = sb("r_eselG", [128, G_, 8], F32)
            mk1G = sb("r_mk1G", [128, G_, 8], F32)
            mk2G = sb("r_mk2G", [128, G_, 8], F32)
            e2G = sb("r_e2G", [128, G_, 8], F32)
            AbG = sb("r_AbG", [128, G_, 32], BF16)
            t2G = sb("r_t2G", [128, G_, 2, 32], F32)
            asum = sb("r_asum", [128, 32], F32)
            posG = sb("r_posG", [128, G_, 2], F32)
            b3 = lambda ap, n: ap.unsqueeze(2).broadcast_to([128, G_, n])
            S.dma("sp", xs[0][:], self.X1[0:128, :], [], [("xs", 0)], ("xs", 0))
            for g0 in range(0, nt, G_):
                for gi in range(G_):
                    t = g0 + gi
                    sl = t % 2
                    if t + 1 < nt:
                        S.dma("sp", xs[1 - sl][:], self.X1[(t + 1) * 128:(t + 2) * 128, :], [], [("xs", 1 - sl)], ("xs", 1 - sl))
                    for kc in range(8):
                        S.tr(pdn[:, kc // 4, (kc % 4) * 128:(kc % 4 + 1) * 128], xs[sl][:, kc * 128:(kc + 1) * 128], self.ident_f[:],
                             [("xs", sl), "ident_f"], [("pdn", kc // 4)])
                    S.cp("act", x1T[sl][:, 0:4, :], pdn[:, 0, :].rearrange("p (a b) -> p a b", a=4), [("pdn", 0)], [("x1T", sl, 0)])
                    S.cp("dve", x1T[sl][:, 4:8, :], pdn[:, 1, :].rearrange("p (a b) -> p a b", a=4), [("pdn", 1)], [("x1T", sl, 1)])
                    for kc in range(8):
                        S.mm(pdn[:, 2, gi * 36:(gi + 1) * 36], x1T[sl][:, kc, :], wr[:, kc, :], kc == 0, kc == 7,
                             [("x1T", sl, kc // 4), "wr"], [("pdn", 2)])
                S.cp("act", rlG[:].rearrange("p t n -> p (t n)"), pdn[:, 2, 0:G_ * 36], [("pdn", 2)], ["rl"])
                lg = rlG[:, :, 0:4]
                le = rlG[:, :, 4:36].rearrange("p t (g e) -> p t g e", g=4)
                S.op("dve", lambda lg=lg: V.reduce_max(out=f2["gmax"][:], in_=lg, axis=AX.X), ["rl"], ["gmax"])
                S.tt("dve", ohG[:], lg, b3(f2["gmax"][:], 4), ALU.is_equal, ["rl", "gmax"], ["oh"])
                S.tt("dve", lgsG[:], lg, b3(f2["gmax"][:], 4), ALU.subtract, ["rl", "gmax"], ["lgs"])
                S.act(egG[:], lgsG[:], AF.Exp, ["lgs"], ["eg"])
                S.op("dve", lambda: V.reduce_sum(out=f2["gsum"][:], in_=egG[:], axis=AX.X), ["eg"], ["gsum"])
                S.op("dve", lambda: V.reciprocal(out=f2["gp"][:], in_=f2["gsum"][:]), ["gsum"], ["gp"])
                S.tt("dve", t3G[:], le, ohG[:].unsqueeze(3).broadcast_to([128, G_, 4, 8]), ALU.mult, ["rl", "oh"], ["t3"])
                S.op("dve", lambda: V.tensor_reduce(out=eselG[:], in_=t3G[:].rearrange("p t g e -> p t e g"), axis=AX.X, op=ALU.add),
                     ["t3"], ["esel"])
                S.op("dve", lambda: V.reduce_max(out=f2["m1"][:], in_=eselG[:], axis=AX.X), ["esel"], ["m1"])
                S.tt("dve", mk1G[:], eselG[:], b3(f2["m1"][:], 8), ALU.is_equal, ["esel", "m1"], ["mk1"])
                S.stt("dve", e2G[:], mk1G[:], -1e30, eselG[:], ALU.mult, ALU.add, ["mk1", "esel"], ["e2"])
                S.op("dve", lambda: V.reduce_max(out=f2["m2"][:], in_=e2G[:], axis=AX.X), ["e2"], ["m2"])
                S.tt("dve", mk2G[:], e2G[:], b3(f2["m2"][:], 8), ALU.is_equal, ["e2", "m2"], ["mk2"])
                S.tt("dve", f2["dd"][:], f2["m2"][:], f2["m1"][:], ALU.subtract, ["m1", "m2"], ["dd"])
                S.act(f2["ed"][:], f2["dd"][:], AF.Exp, ["dd"], ["ed"])
                S.ts("dve", f2["den"][:], f2["ed"][:], 1.0, None, ALU.add, None, ["ed"], ["den"])
                S.op("dve", lambda: V.reciprocal(out=f2["w1"][:], in_=f2["den"][:]), ["den"], ["w1r"])
                S.tt("dve", f2["w2"][:], f2["ed"][:], f2["w1"][:], ALU.mult, ["ed", "w1r"], ["w2r"])
                S.tt("dve", WG[:, g0:g0 + G_, 0], f2["w1"][:], f2["gp"][:], ALU.mult, ["w1r", "gp"], [("WG", g0, 0)])
                S.tt("dve", WG[:, g0:g0 + G_, 1], f2["w2"][:], f2["gp"][:], ALU.mult, ["w2r", "gp"], [("WG", g0, 1)])
                ohb = ohG[:].unsqueeze(3).broadcast_to([128, G_, 4, 8])
                S.tt("dve", MK[:, g0:g0 + G_, 0, :].rearrange("p t (g e) -> p t g e", g=4), ohb,
                     mk1G[:].unsqueeze(2).broadcast_to([128, G_, 4, 8]), ALU.mult, ["oh", "mk1"], [("MK", g0, 0)])
                S.tt("dve", MK[:, g0:g0 + G_, 1, :].rearrange("p t (g e) -> p t g e", g=4), ohb,
                     mk2G[:].unsqueeze(2).broadcast_to([128, G_, 4, 8]), ALU.mult, ["oh", "mk2"], [("MK", g0, 1)])
                S.tt("dve", AbG[:], MK[:, g0:g0 + G_, 0, :], MK[:, g0:g0 + G_, 1, :], ALU.add, [("MK", g0, 0), ("MK", g0, 1)], ["Ab"])
                for gi in range(G_):
                    po_ = pdn[:, 3, gi * 32:(gi + 1) * 32]
                    S.mm(po_, U[:], AbG[:, gi, :], True, False, ["U", "Ab"], [("pdn", 3)])
                    S.mm(po_, onesb[:], AcumB[:], False, gi == 0, ["onesb", "AcumB"], [("pdn", 3)])
                    for gp_ in range(gi):
                        S.mm(po_, onesb[:], AbG[:, gp_, :], False, gp_ == gi - 1, ["onesb", "Ab"], [("pdn", 3)])
                pr2 = pdn[:, 3, 0:G_ * 32].rearrange("p (t e) -> p t e", e=32)
                S.tt("dve", t2G[:], MK[:, g0:g0 + G_, :, :], pr2.unsqueeze(2).broadcast_to([128, G_, 2, 32]), ALU.mult,
                     [("MK", g0, 0), ("MK", g0, 1), ("pdn", 3)], ["t2"])
                S.op("dve", (lambda g0=g0: V.reduce_sum(out=RK[:, g0:g0 + G_, :], in_=t2G[:], axis=AX.X)), ["t2"], [("RK", g0)])
                S.op("dve", lambda: V.tensor_reduce(out=asum[:], in_=AbG[:].rearrange("p t e -> p e t"), axis=AX.X, op=ALU.add),
                     ["Ab"], ["asum"])
                S.tt("dve", Acum[:], Acum[:], asum[:], ALU.add, ["Acum", "asum"], ["Acum"])
                S.cp("dve", AcumB[:], Acum[:], ["Acum"], ["AcumB"])
            S.mm(pdn[:, 3, 0:32], onesb[:], AcumB[:], True, True, ["onesb", "AcumB"], [("pdn", 3)])
            S.cp("act", ntot[:], pdn[:, 3, 0:32], [("pdn", 3)], ["ntot"])
            S.ts("dve", thr[:], iota[:], -1.0, float(SLOT), ALU.add, ALU.mult, ["iota"], ["thr"])
            bj = big[:, 0:32 * J].rearrange("p (e j) -> p e j", e=32)
            S.tt("dve", bj, ntot[:].unsqueeze(2).broadcast_to([128, 32, J]), thr[:, 0:J].unsqueeze(1).broadcast_to([128, 32, J]),
                 ALU.is_gt, ["ntot", "thr"], ["big"])
            S.op("dve", lambda: V.reduce_sum(out=nte[:], in_=bj, axis=AX.X), ["big"], ["nte"])
            be = big[:, 0:1024].rearrange("p (e f) -> p e f", e=32)
            S.tt("dve", be, iota[:, 0:32].unsqueeze(1).broadcast_to([128, 32, 32]), iota[:, 0:32].unsqueeze(2).broadcast_to([128, 32, 32]),
                 ALU.is_le, ["iota", "nte"], ["big2"])
            S.tt("dve", be, be, nte[:].unsqueeze(1).broadcast_to([128, 32, 32]), ALU.mult, ["big2", "nte"], ["big3"])
            S.op("dve", lambda: V.reduce_sum(out=cinc[:], in_=be, axis=AX.X), ["big3"], ["cinc"])
            S.tt("dve", base[:], cinc[:], nte[:], ALU.subtract, ["cinc", "nte"], ["base0"])
            S.ts("dve", base[:], base[:], float(SLOT), None, ALU.mult, None, ["base0"], ["base"])
            bs = big[:, 0:nslot * 32].rearrange("p (s e) -> p s e", e=32)
            S.tt("dve", bs, cinc[:].unsqueeze(1).broadcast_to([128, nslot, 32]), iota[:, 0:nslot].unsqueeze(2).broadcast_to([128, nslot, 32]),
                 ALU.is_lt, ["cinc", "iota", "big3"], ["big4"])
            S.op("dve", lambda: V.reduce_sum(out=eidf[:, 0:nslot], in_=bs, axis=AX.X), ["big4"], ["eidf0"])
            S.ts("dve", eidf[:, 0:nslot], eidf[:, 0:nslot], 31.0, None, ALU.min, None, ["eidf0"], ["eidf"])
            S.ts("dve", eidf[:, 0:nslot], eidf[:, 0:nslot], 128.0, pcol[:, 1:2], ALU.mult, ALU.add, ["eidf", "pcol"], ["eidf2"])
            S.cp("dve", eidi[:, 0:nslot], eidf[:, 0:nslot], ["eidf2"], ["eidi"])
            def dump():
                if "ROUTE" in self.dbg and not hasattr(self, "_dumped"):
                    self._dumped = True
                    d_pi = self.nc.dram_tensor("DBG_PI", [128, nt * 2], I32, kind="ExternalOutput").ap()
                    d_ei = self.nc.dram_tensor("DBG_EI", [128, 128], I32, kind="ExternalOutput").ap()
                    d_wg = self.nc.dram_tensor("DBG_WG", [128, nt * 2], F32, kind="ExternalOutput").ap()
                    d_nt = self.nc.dram_tensor("DBG_NT", [128, 32], F32, kind="ExternalOutput").ap()
                    d_rk = self.nc.dram_tensor("DBG_RK", [128, nt * 2], F32, kind="ExternalOutput").ap()
                    S.dma("sp", d_pi, PI[:].rearrange("p t k -> p (t k)"), [("PI", t_) for t_ in range(0, nt, G_)], [], "dbg0")
                    S.dma("sp", d_ei, eidi[:], ["eidi"], [], "dbg1")
                    S.dma("sp", d_wg, WG[:].rearrange("p t k -> p (t k)"), [], [], "dbg2")
                    S.dma("sp", d_nt, ntot[:], ["ntot"], [], "dbg3")
                    S.dma("sp", d_rk, RK[:].rearrange("p t k -> p (t k)"), [("RK", t_) for t_ in range(0, nt, G_)], [], "dbg4")
            for g0 in range(0, nt, G_):
                S.tt("dve", t2G[:], MK[:, g0:g0 + G_, :, :], base[:].unsqueeze(1).unsqueeze(1).broadcast_to([128, G_, 2, 32]), ALU.mult,
                     [("MK", g0, 0), ("MK", g0, 1), "base"], ["t2"])
                S.op("dve", lambda: V.reduce_sum(out=posG[:], in_=t2G[:], axis=AX.X), ["t2"], ["posf0"])
                S.tt("dve", posG[:], posG[:], RK[:, g0:g0 + G_, :], ALU.add, ["posf0", ("RK", g0)], ["posf"])
                S.cp("dve", PI[:, g0:g0 + G_, :], posG[:], ["posf"], [("PI", g0)])
            S.dma("sp", xb[0][:], self.X1B[0:128, :], [], [("xb", 0)], ("xb", 0))
            for t in range(nt):
                sl = t % 2
                if t + 1 < nt:
                    S.dma("sp", xb[1 - sl][:], self.X1B[(t + 1) * 128:(t + 2) * 128, :], [], [("xb", 1 - sl)], ("xb", 1 - sl))
                for k in range(2):
                    if self.moe_stop == "A0":
                        continue
                    S.op("pool", (lambda t=t, k=k, sl=sl: nc.gpsimd.indirect_dma_start(
                        out=self.XS[:, :], out_offset=bass.IndirectOffsetOnAxis(ap=PI[:, t, k:k + 1], axis=0),
                        in_=xb[sl][:], in_offset=None)), [("xb", sl), ("PI", (t // G_) * G_)], [], dma=True, sk=("scat", sl, k))
            dump()
            S.flush()
            if self.moe_stop in ("A0", "A"):
                return


            def load_slot(si_):
                es = si_ % 2
                S.op("pool", (lambda es=es, si_=si_: nc.gpsimd.indirect_dma_start(
                    out=gu[es][:].rearrange("p k n -> p (k n)"), out_offset=None, in_=self.WGUB[:, :],
                    in_offset=bass.IndirectOffsetOnAxis(ap=eidi[:, si_:si_ + 1], axis=0))), [], [("gu", es)], dma=True, sk=("gu", es))
                S.op("pool", (lambda es=es, si_=si_: nc.gpsimd.indirect_dma_start(
                    out=dn[es][:].rearrange("p k n -> p (k n)"), out_offset=None, in_=self.WDNB[:, :],
                    in_offset=bass.IndirectOffsetOnAxis(ap=eidi[:, si_:si_ + 1], axis=0))), [], [("dn", es)], dma=True, sk=("dn", es))
                for q in range(2):
                    S.dma("sp", xsl[es][q][:], self.XS[si_ * SLOT + q * 128:si_ * SLOT + (q + 1) * 128, :], [], [("xsl", es, q)], ("xsl", es, q))

            load_slot(0)
            for si_ in range(nslot):
                es = si_ % 2
                if si_ + 1 < nslot:
                    load_slot(si_ + 1)
                for q in range(2):
                    for kc in range(8):
                        S.tr(pq[:, q, kc, :], xsl[es][q][:, kc * 128:(kc + 1) * 128], self.ident_b[:], [("xsl", es, q), "ident_b"], [("pq", q)])
                    S.cp("act" if q == 0 else "dve", xT[es][:, :, q * 128:(q + 1) * 128], pq[:, q], [("pq", q)], [("xT", es, q)])
                for j in range(4):
                    pp = j % 2
                    pa = pgu[:, pp, 0:SLOT]
                    pg = pgu[:, pp, SLOT:2 * SLOT]
                    for kc in range(8):
                        S.mm(pa, gu[es][:, kc, j * 128:(j + 1) * 128], xT[es][:, kc, :], kc == 0, kc == 7,
                             [("gu", es), ("xT", es, 0), ("xT", es, 1)], [("pgu", pp)])
                    for kc in range(8):
                        S.mm(pg, gu[es][:, kc, 512 + j * 128:512 + (j + 1) * 128], xT[es][:, kc, :], kc == 0, kc == 7,
                             [("gu", es), ("xT", es, 0), ("xT", es, 1)], [("pgu", pp)])
                    S.act(slu[pp][:], pa, AF.Silu, [("pgu", pp)], [("slu", pp)])
                    S.tt("dve", actT[es][:, j, :], slu[pp][:], pg, ALU.mult, [("slu", pp), ("pgu", pp)], [("actT", es, j)])
                for q in range(2):
                    for cb in range(2):
                        for j in range(4):
                            S.mm(pdn[:, 2 * q + cb, :], actT[es][:, j, q * 128:(q + 1) * 128], dn[es][:, j, cb * 512:(cb + 1) * 512],
                                 j == 0, j == 3, [("actT", es, j), ("dn", es)], [("pdn", 2 * q + cb)])
                    S.cp("act", yo[q][:], pdn[:, 2 * q:2 * q + 2, :].rearrange("p a b -> p (a b)"), [("pdn", 2 * q), ("pdn", 2 * q + 1)], [("yo", q)])
                    S.dma("sp", self.YS[si_ * SLOT + q * 128:si_ * SLOT + (q + 1) * 128, :], yo[q][:], [("yo", q)], [], ("st_yo", q))
            S.flush()
            if self.moe_stop == "B":
                return

            def load_c(t):
                sl = t % 2
                S.dma("sp", xs[sl][:], self.X1[t * 128:(t + 1) * 128, :], [], [("xs", sl)], ("xs", sl))
                for k, gb_ in enumerate((g1, g2)):
                    S.op("pool", (lambda t=t, k=k, sl=sl, gb_=gb_: nc.gpsimd.indirect_dma_start(
                        out=gb_[sl][:], out_offset=None, in_=self.YS[:, :],
                        in_offset=bass.IndirectOffsetOnAxis(ap=PI[:, t, k:k + 1], axis=0))), [], [("g", k, sl)], dma=True, sk=("gath", sl, k))

            load_c(0)
            for t in range(nt):
                sl = t % 2
                if t + 1 < nt:
                    load_c(t + 1)
                S.ts("dve", g1[sl][:], g1[sl][:], WG[:, t, 0:1], None, ALU.mult, None, [("g", 0, sl)], [("g", 0, sl)])
                S.stt("dve", g2[sl][:], g2[sl][:], WG[:, t, 1:2], g1[sl][:], ALU.mult, ALU.add, [("g", 1, sl), ("g", 0, sl)], [("g", 1, sl)])
                S.stt("dve", r[:], xs[sl][:], alpha, g2[sl][:], ALU.mult, ALU.add, [("xs", sl), ("g", 1, sl)], ["r"])
                self._ln("dve", r, stats, mv, rstd, nb, xn, G, Bv, xo[sl][:], ["r"], [("xo", sl)], tail="dve")
                xd_, tq_ = tile_dst[t]
                S.dma("sp", xd_[tq_ * 128:(tq_ + 1) * 128, :], xo[sl][:], [("xo", sl)], [], ("st_xo", sl))
            S.flush()


    def p4b_dense(self, l, s, xsrc, xdst):
        nc, S = self.nc, self.S
        alpha = self.alpha
        SB = min(s, 2048)
        BK = min(SB, 512)
        ntb = SB // 128
        nblk = SB // BK
        tpb = BK // 128
        with ExitStack() as st:
            sb = lambda name, shape, dt: self.sb(st, name, shape, dt)
            wr = sb("p5_wr", [128, 8, 36], BF16)
            S.dma("pool", wr[:], self.router[l].rearrange("(kc p) n -> p kc n", p=128), [], ["wr"], "wr")
            G = sb("p5_G", [128, 1024], F32)
            Bv = sb("p5_B", [128, 1024], F32)
            S.dma("sp", G[:], self.ln2_g[l:l + 1, :].broadcast_to([128, 1024]), [], ["LNG"], "LNG")
            S.dma("sp", Bv[:], self.ln2_b[l:l + 1, :].broadcast_to([128, 1024]), [], ["LNB"], "LNB")
            x1T = sb("p5_x1T", [128, 8, SB], BF16)
            yacc = sb("p5_yacc", [128, ntb, 1024], F32)
            gu = [sb(f"p5_gu{i}", [128, 8, 1024], BF16) for i in range(2)]
            dn = [sb(f"p5_dn{i}", [128, 4, 1024], BF16) for i in range(2)]
            actT = [sb(f"p5_actT{i}", [128, 4, BK], BF16) for i in range(2)]
            slu = [sb(f"p5_slu{i}", [128, BK], F32) for i in range(2)]
            C = sb("p5_C", [128, ntb, 32], F32)
            xs = [sb(f"p5_xs{i}", [128, 1024], F32) for i in range(2)]
            r = sb("p5_r", [128, 1024], F32)
            xn = sb("p5_xn", [128, 1024], F32)
            xo = [sb(f"p5_xo{i}", [128, 1024], F32) for i in range(2)]
            stats = sb("p5_stats", [128, 2, 6], F32)
            mv = sb("p5_mv", [128, 2], F32)
            rstd = sb("p5_rstd", [128, 1], F32)
            nb = sb("p5_nb", [128, 1], F32)
            rl = sb("p5_rl", [128, 36], F32)
            sm = {nm: sb("p5_" + nm, [128, 1], F32) for nm in
                  ("gmax", "ngmax", "gsum", "gp", "m1", "m2", "dd", "ed", "den", "w1", "w2")}
            oh = sb("p5_oh", [128, 4], F32)
            eg = sb("p5_eg", [128, 4], F32)
            t3 = sb("p5_t3", [128, 4, 8], F32)
            esel = sb("p5_esel", [128, 8], F32)
            mk1 = sb("p5_mk1", [128, 8], F32)
            mk2 = sb("p5_mk2", [128, 8], F32)
            e2 = sb("p5_e2", [128, 8], F32)
            cl = sb("p5_cl", [128, 8], F32)
            pgu = self.pst(st, "p5_pgu", [128, 4, 512], F32)
            pdn = self.pst(st, "p5_pdn", [128, 4, 512], F32)
            ecount = 0

            def load_w(e, es):
                guv = self.w_gate_up[l, e].rearrange("(kc p) n -> p kc n", p=128)
                for q4 in range(4):
                    S.dma("pool", gu[es][:, 2 * q4:2 * q4 + 2, :], guv[:, 2 * q4:2 * q4 + 2, :], [], [("gu", es, q4)], ("gu", es, q4))
                S.dma("pool", dn[es][:], self.w_down[l, e].rearrange("(j p) n -> p j n", p=128), [], [("dn", es)], ("dn", es))

            for sbi in range(s // SB):
                base = sbi * SB
                load_w(0, ecount % 2)
                S.dma("sp", xs[0][:], self.X1[base:base + 128, :], [], [("xs", 0)], ("xs", 0))
                for tl in range(ntb):
                    sl = tl % 2
                    if tl + 1 < ntb:
                        S.dma("sp", xs[1 - sl][:], self.X1[base + (tl + 1) * 128:base + (tl + 2) * 128, :], [], [("xs", 1 - sl)], ("xs", 1 - sl))
                    for kc in range(8):
                        S.tr(pgu[:, kc // 4, (kc % 4) * 128:(kc % 4 + 1) * 128], xs[sl][:, kc * 128:(kc + 1) * 128], self.ident_f[:],
                             [("xs", sl), "ident_f"], [("pgu", kc // 4)])
                    cols = slice(tl * 128, (tl + 1) * 128)
                    S.cp("act", x1T[:, 0:4, cols], pgu[:, 0, :].rearrange("p (a b) -> p a b", a=4), [("pgu", 0)], [("x1T", tl, 0)])
                    S.cp("dve", x1T[:, 4:8, cols], pgu[:, 1, :].rearrange("p (a b) -> p a b", a=4), [("pgu", 1)], [("x1T", tl, 1)])
                    for kc in range(8):
                        S.mm(pdn[:, 0, 0:36], x1T[:, kc, cols], wr[:, kc, :], kc == 0, kc == 7, [("x1T", tl, kc // 4), "wr"], [("pdn", 0)])
                    S.cp("act", rl[:], pdn[:, 0, 0:36], [("pdn", 0)], ["rl"])
                    V = nc.vector
                    S.op("dve", lambda: V.reduce_max(out=sm["gmax"][:], in_=rl[:, 0:4], axis=AX.X), ["rl"], ["gmax"])
                    S.ts("dve", oh[:], rl[:, 0:4], sm["gmax"][:, 0:1], None, ALU.is_equal, None, ["rl", "gmax"], ["oh"])
                    S.ts("dve", sm["ngmax"][:], sm["gmax"][:], -1.0, None, ALU.mult, None, ["gmax"], ["ngmax"])
                    S.act(eg[:], rl[:, 0:4], AF.Exp, ["rl", "ngmax"], ["eg"], bias=sm["ngmax"][:, 0:1], accum=sm["gsum"][:])
                    S.op("dve", lambda: V.reciprocal(out=sm["gp"][:], in_=sm["gsum"][:]), ["eg"], ["gp"])
                    S.tt("dve", t3[:], rl[:, 4:36].rearrange("p (g e) -> p g e", g=4), oh[:].unsqueeze(2).broadcast_to([128, 4, 8]),
                         ALU.mult, ["rl", "oh"], ["t3"])
                    S.op("dve", lambda: V.tensor_reduce(out=esel[:], in_=t3[:].rearrange("p g e -> p e g"), axis=AX.X, op=ALU.add),
                         ["t3"], ["esel"])
                    S.op("dve", lambda: V.reduce_max(out=sm["m1"][:], in_=esel[:], axis=AX.X), ["esel"], ["m1"])
                    S.ts("dve", mk1[:], esel[:], sm["m1"][:, 0:1], None, ALU.is_equal, None, ["esel", "m1"], ["mk1"])
                    S.stt("dve", e2[:], mk1[:], -1e30, esel[:], ALU.mult, ALU.add, ["mk1", "esel"], ["e2"])
                    S.op("dve", lambda: V.reduce_max(out=sm["m2"][:], in_=e2[:], axis=AX.X), ["e2"], ["m2"])
                    S.ts("dve", mk2[:], e2[:], sm["m2"][:, 0:1], None, ALU.is_equal, None, ["e2", "m2"], ["mk2"])
                    S.tt("dve", sm["dd"][:], sm["m2"][:], sm["m1"][:], ALU.subtract, ["m1", "m2"], ["dd"])
                    S.act(sm["ed"][:], sm["dd"][:], AF.Exp, ["dd"], ["ed"])
                    S.ts("dve", sm["den"][:], sm["ed"][:], 1.0, None, ALU.add, None, ["ed"], ["den"])
                    S.op("dve", lambda: V.reciprocal(out=sm["w1"][:], in_=sm["den"][:]), ["den"], ["w1r"])
                    S.tt("dve", sm["w2"][:], sm["ed"][:], sm["w1"][:], ALU.mult, ["ed", "w1r"], ["w2r"])
                    S.tt("dve", sm["w1"][:], sm["w1"][:], sm["gp"][:], ALU.mult, ["w1r", "gp", "w2r"], ["w1"])
                    S.tt("dve", sm["w2"][:], sm["w2"][:], sm["gp"][:], ALU.mult, ["w2r", "gp"], ["w2"])
                    S.ts("dve", cl[:], mk1[:], sm["w1"][:, 0:1], None, ALU.mult, None, ["mk1", "w1"], ["cl0"])
                    S.stt("dve", cl[:], mk2[:], sm["w2"][:, 0:1], cl[:], ALU.mult, ALU.add, ["mk2", "w2", "cl0"], ["cl"])
                    S.tt("dve", C[:, tl, :].rearrange("p (g e) -> p g e", g=4), oh[:].unsqueeze(2).broadcast_to([128, 4, 8]),
                         cl[:].unsqueeze(1).broadcast_to([128, 4, 8]), ALU.mult, ["oh", "cl"], [("C", tl)])
                for e in range(N_EXP):
                    es = ecount % 2
                    if e + 1 < N_EXP:
                        load_w(e + 1, (ecount + 1) % 2)
                    for blk in range(nblk):
                        a_s = blk % 2
                        bcols = slice(blk * BK, (blk + 1) * BK)
                        for j in range(4):
                            pp = j % 2
                            pa = pgu[:, 2 * pp, 0:BK]
                            pg = pgu[:, 2 * pp + 1, 0:BK]
                            for kc in range(8):
                                S.mm(pa, gu[es][:, kc, j * 128:(j + 1) * 128], x1T[:, kc, bcols], kc == 0, kc == 7,
                                     [("gu", es, kc // 2)] + [("x1T", blk * tpb + q, kc // 4) for q in range(tpb)], [("pgu", 2 * pp)])
                            for kc in range(8):
                                S.mm(pg, gu[es][:, kc, 512 + j * 128:512 + (j + 1) * 128], x1T[:, kc, bcols], kc == 0, kc == 7,
                                     [("gu", es, kc // 2)] + [("x1T", blk * tpb + q, kc // 4) for q in range(tpb)], [("pgu", 2 * pp + 1)])
                            S.act(slu[pp][:], pa, AF.Silu, [("pgu", 2 * pp)], [("slu", pp)])
                            S.tt("dve", actT[a_s][:, j, :], slu[pp][:], pg, ALU.mult, [("slu", pp), ("pgu", 2 * pp + 1)], [("actT", a_s, j)])
                        for q in range(tpb):
                            tl = blk * tpb + q
                            dp = q % 2
                            for cb in range(2):
                                for j in range(4):
                                    S.mm(pdn[:, 2 * dp + cb, :], actT[a_s][:, j, q * 128:(q + 1) * 128], dn[es][:, j, cb * 512:(cb + 1) * 512],
                                         j == 0, j == 3, [("actT", a_s, j), ("dn", es)], [("pdn", 2 * dp + cb)])
                            pv = pdn[:, 2 * dp:2 * dp + 2, :].rearrange("p a b -> p (a b)")
                            pk = [("pdn", 2 * dp), ("pdn", 2 * dp + 1)]
                            if e == 0:
                                S.ts("dve", yacc[:, tl, :], pv, C[:, tl, e:e + 1], None, ALU.mult, None, pk + [("C", tl)], [("yacc", tl)])
                            else:
                                S.stt("dve", yacc[:, tl, :], pv, C[:, tl, e:e + 1], yacc[:, tl, :], ALU.mult, ALU.add,
                                      pk + [("C", tl), ("yacc", tl)], [("yacc", tl)])
                    ecount += 1
                S.dma("sp", xs[0][:], self.X1[base:base + 128, :], [], [("xs", 0)], ("xs", 0))
                for tl in range(ntb):
                    sl = tl % 2
                    if tl + 1 < ntb:
                        S.dma("sp", xs[1 - sl][:], self.X1[base + (tl + 1) * 128:base + (tl + 2) * 128, :], [], [("xs", 1 - sl)], ("xs", 1 - sl))
                    S.stt("dve", r[:], xs[sl][:], alpha, yacc[:, tl, :], ALU.mult, ALU.add, [("xs", sl), ("yacc", tl)], ["r"])
                    self._ln("dve", r, stats, mv, rstd, nb, xn, G, Bv, xo[sl][:], ["r"], [("xo", sl)])
                    S.dma("sp", xdst[base + tl * 128:base + (tl + 1) * 128, :], xo[sl][:], [("xo", sl)], [], ("st_xo", sl))
            S.flush()


def prep_inputs(inputs, s_list, depth, names=("x_prompt", "x_sample")):
    L = depth
    f = lambda a: np.ascontiguousarray(np.asarray(a, dtype=np.float32))
    shared = dict(
        w_in=f(inputs["w_in"][:L]),
        ret_log_decay=f(inputs["ret_log_decay"][:L]).reshape(L, 8),
        ret_gn_gain=f(inputs["ret_gn_gain"][:L]),
        diff_lambda=f(inputs["diff_lambda"][:L]).reshape(L, 512),
        diff_subln_gain=f(inputs["diff_subln_gain"][:L]),
        w_ret_branch=f(inputs["w_ret_branch"][:L]),
        w_diff_branch=f(inputs["w_diff_branch"][:L]),
        w_out=f(inputs["w_out"][:L]),
        ln1_g=f(inputs["ln1_g"][:L]), ln1_b=f(inputs["ln1_b"][:L]),
        router=np.ascontiguousarray(np.concatenate([f(inputs["router_group"][:L]), f(inputs["router_expert"][:L])], axis=-1)),
        w_gate_up=f(inputs["w_gate_up"][:L]), w_down=f(inputs["w_down"][:L]),
        ln2_g=f(inputs["ln2_g"][:L]), ln2_b=f(inputs["ln2_b"][:L]),
    )
    shared.update(host_constants(max(s_list)))
    maps = []
    for c in range(NCORES):
        m = dict(shared)
        for i, nm in enumerate(names):
            m[f"x{i}"] = f(inputs[nm][c])
        maps.append(m)
    return maps


def kernel(**inputs):
    s_list = [inputs["x_prompt"].shape[1], inputs["x_sample"].shape[1]]
    depth = inputs["w_in"].shape[0]
    b = Builder(s_list, depth)
    nc = b.build()
    maps = prep_inputs(inputs, s_list, depth)
    res = run_bass_kernel_spmd(nc, maps, core_ids=list(range(NCORES)))
    outs = []
    for i in range(2):
        outs.append(np.stack([np.asarray(res.results[c][f"y{i}"], dtype=np.float32) for c in range(NCORES)], axis=0))
    return tuple(outs)
```
